# Optimizing a Trainium2 kernel written in Bass

```python
import jax, jax.numpy as jnp
from jax import lax
import numpy as np

D_MODEL = 1024
BATCH = 8
SEQ = 2048
DEPTH = 1
DEC_BATCH = 128
DEC_SEQ = 1
PAST_LEN = 16384
PAGE_SIZE = 128

N_META = 16
POOL_WINDOWS = (2, 4, 8, 16)
N_POOL_GROUPS = len(POOL_WINDOWS)
POOL_WIDTH = D_MODEL // 2
POOL_GC = POOL_WIDTH // N_POOL_GROUPS
POOL_OUT_GC = D_MODEL // N_POOL_GROUPS
POOL_BUF = max(POOL_WINDOWS) - 1
CONV_WIDTH = D_MODEL // 2
CONV_W = 3
D_FF = 2816
EPS = 1e-6
IN_SIZES = (POOL_WIDTH, CONV_WIDTH, CONV_WIDTH, CONV_WIDTH, D_MODEL, D_MODEL)
IN_SPLITS = tuple(int(s) for s in np.cumsum(IN_SIZES)[:-1])
IN_TOTAL = int(sum(IN_SIZES))

kernel_name = "pool_shortconv_gated_hybrid_step"


def rmsnorm(x, g):
    xf = x.astype(jnp.float32)
    y = xf * lax.rsqrt(jnp.mean(xf * xf, axis=-1, keepdims=True) + EPS)
    return (y * g.astype(jnp.float32)).astype(x.dtype)


def causal_dwconv(x, buf, w):
    T = x.shape[1]
    xp = jnp.concatenate([buf.astype(x.dtype), x], axis=1)
    y = xp[:, 0:T] * w[0]
    for k in range(1, CONV_W):
        y = y + xp[:, k:k + T] * w[k]
    return y, xp[:, T:]


def multiscale_pool(u, buf, pos0, w_pool, scale):
    B, T, _ = u.shape
    up = jnp.concatenate([buf.astype(u.dtype), u], axis=1)
    cs = jnp.cumsum(up.astype(jnp.float32), axis=1)
    cs = jnp.pad(cs, ((0, 0), (1, 0), (0, 0)))
    hi = cs[:, POOL_BUF + 1:POOL_BUF + 1 + T]
    pos = (pos0 + jnp.arange(T)).astype(jnp.int32)
    uf = u.astype(jnp.float32)
    outs = []
    for g, w in enumerate(POOL_WINDOWS):
        sl = slice(g * POOL_GC, (g + 1) * POOL_GC)
        lo = cs[:, POOL_BUF + 1 - w:POOL_BUF + 1 - w + T, sl]
        cnt = jnp.minimum(w, pos + 1).astype(jnp.float32)[None, :, None]
        outs.append((hi[..., sl] - lo) / cnt - uf[..., sl])
    pooled = jnp.stack(outs, axis=2).astype(u.dtype)
    mixed = jnp.einsum('btgc,gco->btgo', pooled, w_pool).reshape(B, T, D_MODEL)
    return mixed * scale, up[:, T:]


def trunk_layer(x, pos0, buf_pool, buf_conv, buf_ffn, w_in, w_pool, pool_scale, w_conv,
                w_conv_out, w_o, w_up, w_ffn_conv, w_down, g_pre_mix, g_post_mix,
                g_pre_ffn, g_post_ffn):
    h = rmsnorm(x, g_pre_mix)
    z = h @ w_in
    u, v, bg, cg, ga, gb = jnp.split(z, IN_SPLITS, axis=-1)
    a, new_pool = multiscale_pool(u, buf_pool, pos0, w_pool, pool_scale)
    cv, new_conv = causal_dwconv(cg * v, buf_conv, w_conv)
    b = (bg * cv) @ w_conv_out
    merged = jax.nn.sigmoid(ga) * a + jax.nn.sigmoid(gb) * b
    x = x + rmsnorm(merged @ w_o, g_post_mix)
    h = rmsnorm(x, g_pre_ffn)
    upc, new_ffn = causal_dwconv(h @ w_up, buf_ffn, w_ffn_conv)
    gate, val = jnp.split(upc, 2, axis=-1)
    y = (jax.nn.silu(gate) * val) @ w_down
    x = x + rmsnorm(y, g_post_ffn)
    return x, new_pool, new_conv, new_ffn


def setup_inputs(seed: int = 0) -> dict:
    key = jax.random.key(seed)
    ks = jax.random.split(key, 20)
    f32 = jnp.float32
    nrm = lambda k, shape, s: jax.random.normal(k, shape, f32) * s
    return {
        "x_prompt": nrm(ks[0], (BATCH, SEQ, D_MODEL), 1.0),
        "x_sample": nrm(ks[1], (DEC_BATCH, DEC_SEQ, D_MODEL), 1.0),
        "state_pool": nrm(ks[2], (DEPTH, DEC_BATCH, POOL_BUF, POOL_WIDTH), 1.0),
        "state_conv": nrm(ks[3], (DEPTH, DEC_BATCH, CONV_W - 1, CONV_WIDTH), 1.0),
        "state_ffn": nrm(ks[4], (DEPTH, DEC_BATCH, CONV_W - 1, 2 * D_FF), 1.0),
        "meta_tokens": nrm(ks[5], (N_META, D_MODEL), 1.0),
        "w_in": nrm(ks[6], (DEPTH, D_MODEL, IN_TOTAL), D_MODEL ** -0.5),
        "w_pool": nrm(ks[7], (DEPTH, N_POOL_GROUPS, POOL_GC, POOL_OUT_GC), POOL_GC ** -0.5),
        "pool_scale": 1.0 + nrm(ks[8], (DEPTH, D_MODEL), 0.05),
        "w_conv": nrm(ks[9], (DEPTH, CONV_W, CONV_WIDTH), CONV_W ** -0.5),
        "w_conv_out": nrm(ks[10], (DEPTH, CONV_WIDTH, D_MODEL), CONV_WIDTH ** -0.5),
        "w_o": nrm(ks[11], (DEPTH, D_MODEL, D_MODEL), D_MODEL ** -0.5),
        "w_up": nrm(ks[12], (DEPTH, D_MODEL, 2 * D_FF), D_MODEL ** -0.5),
        "w_ffn_conv": nrm(ks[13], (DEPTH, CONV_W, 2 * D_FF), CONV_W ** -0.5),
        "w_down": nrm(ks[14], (DEPTH, D_FF, D_MODEL), D_FF ** -0.5),
        "g_pre_mix": 1.0 + nrm(ks[15], (DEPTH, D_MODEL), 0.05),
        "g_post_mix": 1.0 + nrm(ks[16], (DEPTH, D_MODEL), 0.05),
        "g_pre_ffn": 1.0 + nrm(ks[17], (DEPTH, D_MODEL), 0.05),
        "g_post_ffn": 1.0 + nrm(ks[18], (DEPTH, D_MODEL), 0.05),
    }


def reference(x_prompt, x_sample, state_pool, state_conv, state_ffn, meta_tokens, w_in, w_pool,
              pool_scale, w_conv, w_conv_out, w_o, w_up, w_ffn_conv, w_down, g_pre_mix,
              g_post_mix, g_pre_ffn, g_post_ffn):
    meta = jnp.broadcast_to(meta_tokens.astype(x_prompt.dtype)[None], (BATCH, N_META, D_MODEL))
    xp = jnp.concatenate([meta, x_prompt], axis=1)
    xs = x_sample
    zp_pool = jnp.zeros((BATCH, POOL_BUF, POOL_WIDTH), xp.dtype)
    zp_conv = jnp.zeros((BATCH, CONV_W - 1, CONV_WIDTH), xp.dtype)
    zp_ffn = jnp.zeros((BATCH, CONV_W - 1, 2 * D_FF), xp.dtype)
    pp, pc, pf, sp, sc, sf = [], [], [], [], [], []
    for l in range(DEPTH):
        params = (w_in[l], w_pool[l], pool_scale[l], w_conv[l], w_conv_out[l], w_o[l], w_up[l],
                  w_ffn_conv[l], w_down[l], g_pre_mix[l], g_post_mix[l], g_pre_ffn[l],
                  g_post_ffn[l])
        xp, np_, nc_, nf_ = trunk_layer(xp, 0, zp_pool, zp_conv, zp_ffn, *params)
        pp.append(np_); pc.append(nc_); pf.append(nf_)
        xs, ns_p, ns_c, ns_f = trunk_layer(xs, PAST_LEN, state_pool[l], state_conv[l],
                                           state_ffn[l], *params)
        sp.append(ns_p); sc.append(ns_c); sf.append(ns_f)
    y_prompt = xp[:, N_META:]
    y_sample = xs
    return (y_prompt, y_sample, jnp.stack(pp), jnp.stack(pc), jnp.stack(pf),
            jnp.stack(sp), jnp.stack(sc), jnp.stack(sf))
```

```python
import numpy as np
import concourse.bass as bass
import concourse.mybir as mybir
from concourse.bass_utils import run_bass_kernel_spmd

F32 = mybir.dt.float32
BF16 = mybir.dt.bfloat16
AF = mybir.ActivationFunctionType
ALU = mybir.AluOpType
AX = mybir.AxisListType

NCORES = 8
NT = 8
N = 260
NH = N + 2
UH = N + 15
SC = 244
NS = 16
D = 1024
KC = 8
DFF = 2816
FC = 22
EPS = 1e-6
WINS = (2, 4, 8, 16)
SBW = 53120
NPF = 9
STRICT_SAME_ENGINE = True

WIN_ORDER = list(range(4, 8)) + list(range(12, 16)) + list(range(0, 4)) + list(range(16, 24)) \
    + list(range(24, 32)) + list(range(8, 12))

C_G1, C_G2, C_G3, C_G4, C_PS = 0, 8, 16, 24, 32
C_WC = 40
C_WF = 52
C_IC = 184
C_END = 248


class _Op:
    __slots__ = ("idx", "eng", "fn", "reads", "writes", "dma", "extra", "deps", "sig", "sem", "val", "nb", "after")


class Sched:
    ENGS = ("pe", "act", "dve", "pool", "sp")

    def __init__(self):
        self.ops = []
        self.dma_ops = []

    def add(self, eng, fn, reads=(), writes=(), dma=False, extra=(), nb=False, after=()):
        op = _Op()
        op.nb = nb
        op.after = tuple(after)
        op.idx = len(self.ops)
        op.eng, op.fn, op.reads, op.writes, op.dma = eng, fn, tuple(reads), tuple(writes), dma
        op.extra = tuple(extra)
        op.deps, op.sig, op.sem, op.val = set(), False, None, 0
        self.ops.append(op)
        if dma:
            self.dma_ops.append(op)
        return op

    def tails(self):
        last = {}
        for op in self.ops:
            if not op.dma and op.fn is not None:
                last[op.eng] = op
        return list(last.values()) + [d for d in self.dma_ops if not d.nb]

    def barrier(self):
        t = self.tails()
        for e in self.ENGS:
            self.add(e, None, extra=t)

    def finalize(self, nc, stack, n_sp=14, n_pool=6):
        ops = self.ops
        last_w, readers = {}, {}
        for op in ops:
            raw, oth = set(), set()
            for t in op.reads:
                if t in last_w:
                    raw.add(last_w[t])
            for t in op.writes:
                if t in last_w:
                    oth.add(last_w[t])
                oth.update(readers.get(t, ()))
            for t in op.after:
                if t in last_w:
                    oth.add(last_w[t])
                oth.update(readers.get(t, ()))
            for x in op.extra:
                raw.add(x.idx)
            for t in op.reads:
                readers.setdefault(t, []).append(op.idx)
            for t in op.writes:
                last_w[t] = op.idx
                readers[t] = []
            deps = set(raw)
            for d in oth:
                dop = ops[d]
                if dop.eng == op.eng and not dop.dma and not op.dma and (not STRICT_SAME_ENGINE or op.eng == "pe"):
                    continue
                deps.add(d)
            deps.discard(op.idx)
            op.deps = deps
        sems = {e: stack.enter_context(nc.semaphore("sem_" + e)) for e in ("pe", "act", "dve", "pool")}
        dsem = {"sp": [stack.enter_context(nc.semaphore("dsp%d" % i)) for i in range(n_sp)],
                "pool": [stack.enter_context(nc.semaphore("dpl%d" % i)) for i in range(n_pool)]}
        cnt = {"sp": 0, "pool": 0}
        prev_use = {}
        uses = {}
        for op in ops:
            if op.dma:
                lst = dsem[op.eng]
                s = lst[cnt[op.eng] % len(lst)]
                cnt[op.eng] += 1
                key = id(s)
                uses[key] = uses.get(key, 0) + 1
                op.sem, op.val, op.sig = s, 16 * uses[key], True
                if key in prev_use:
                    op.deps.add(prev_use[key])
                prev_use[key] = op.idx
        for op in ops:
            for d in op.deps:
                ops[d].sig = True
        ccnt = {e: 0 for e in sems}
        for op in ops:
            if op.dma:
                continue
            if op.sig:
                assert op.fn is not None, "nop cannot signal"
                ccnt[op.eng] += 1
                op.sem, op.val = sems[op.eng], ccnt[op.eng]
        self.sems = sems

    def emit(self, eng, e):
        known = {}
        ops = self.ops
        for op in ops:
            if op.eng != eng:
                continue
            need = {}
            for d in op.deps:
                dop = ops[d]
                k = id(dop.sem)
                if k not in need or need[k][1] < dop.val:
                    need[k] = (dop.sem, dop.val)
            for k, (s, v) in need.items():
                if known.get(k, 0) >= v:
                    continue
                e.wait_ge(s, v)
                known[k] = v
            if op.fn is None:
                continue
            ins = op.fn(e)
            if op.sig:
                ins.then_inc(op.sem, 16 if op.dma else 1)


def build_nc():
    from contextlib import ExitStack
    nc = bass.Bass("TRN2", target_bir_lowering=False)

    def din(name, shape):
        return nc.dram_tensor(name, list(shape), F32, kind="ExternalInput").ap()

    def dout(name, shape):
        return nc.dram_tensor(name, list(shape), F32, kind="ExternalOutput").ap()

    xT = din("xT", [NT, 128, KC * N])
    w_in = din("w_in", [128, 32 * KC * 128])
    w_pool = din("w_pool", [128, 8 * 128])
    w_co = din("w_co", [128, 8 * 4 * 128])
    w_o = din("w_o", [128, 8 * KC * 128])
    w_up = din("w_up", [128, 44 * KC * 128])
    w_dn = din("w_dn", [128, 8 * FC * 128])
    cst = din("cst", [128, 256])
    st_pool_d = din("st_pool", [128, 4 * 15 * NS])
    st_conv_d = din("st_conv", [128, 4 * 2 * NS])
    st_ffn_d = din("st_ffn", [128, 44 * 2 * NS])

    yT = dout("yT", [NT, 128, KC * N])
    o_small = dout("o_small", [128, 196])
    o_ffn_tail = dout("o_ffn_tail", [128, 44 * 18])
    o_pool_sa = dout("o_pool_sa", [128, 4, 14 * NS])
    o_conv_st = dout("o_conv_st", [128, 4 * 2 * NS])
    o_ffn_st = dout("o_ffn_st", [128, 44 * 2 * NS])
    x1T = nc.dram_tensor("x1T", [NT, 128, KC * N], F32).ap()

    stack = ExitStack()
    big = stack.enter_context(nc.sbuf_tensor("big", [128, SBW], F32))
    ps = stack.enter_context(nc.psum_tensor("ps", [128, 4096], F32))

    def R(off, n):
        return big[:, off:off + n]

    def RB(off, n_words):
        return big[:, off:off + n_words].bitcast(BF16)

    def bank(b, n=N):
        return ps[:, b * 512:b * 512 + n]

    CST = R(0, 256)
    o = 256
    ST_POOL = R(o, 960).rearrange("p (c r s) -> p c r s", c=4, r=15); o += 960
    ST_CONV = R(o, 128).rearrange("p (c r s) -> p c r s", c=4, r=2); o += 128
    ST_FFN = R(o, 1408).rearrange("p (c r s) -> p c r s", c=44, r=2); o += 1408
    PRE = R(o, 704).rearrange("p (c s) -> p c s", c=44); o += 704
    RED = R(o, 64).rearrange("p (c s) -> p c s", c=4); o += 64
    ONES = RB(o, 64); o += 64
    SD = [R(o, N), R(o + N, N)]; o += 2 * N
    COMMON_END = o

    AREG = []
    claimed = set()

    def areg(lo, n, tok):
        AREG.append((lo, lo + n, tok))

    def alias(lo, n):
        if (lo, n) in claimed:
            return []
        claimed.add((lo, n))
        return [t for (a, b, t) in AREG if a < lo + n and lo < b]

    areg(256, 960, "st_pool")
    areg(1216, 128, "st_conv")
    o = COMMON_END
    W_IN = RB(o, 16384).rearrange("p (j k c) -> p j k c", j=32, k=KC); WIN_OFF = o; o += 16384
    W_POOL = RB(o, 512).rearrange("p (j c) -> p j c", j=8); WPOOL_OFF = o; areg(o, 512, "w_pool"); o += 512
    W_CO = RB(o, 2048).rearrange("p (j k c) -> p j k c", j=8, k=4); WCO_OFF = o; areg(o, 2048, "w_co"); o += 2048
    W_O = RB(o, 4096).rearrange("p (j k c) -> p j k c", j=8, k=KC); WO_OFF = o
    for j in range(8):
        areg(o + 512 * j, 512, ("w_o", j))
    o += 4096
    XB = []
    for i in range(2):
        XB.append(R(o, KC * N).rearrange("p (k n) -> p k n", k=KC))
        for k in range(KC):
            areg(o + k * N, N, ("x", i, k))
        o += KC * N
    XSQ = RB(o, KC * N // 2).rearrange("p (k n) -> p k n", k=KC); XSQ_OFF = o
    for k in range(KC):
        areg(o + k * (N // 2), N // 2, ("xsq", k))
    o += KC * N // 2
    HB = []
    for i in range(2):
        HB.append(RB(o, KC * N // 2).rearrange("p (k n) -> p k n", k=KC))
        for k in range(KC):
            areg(o + k * (N // 2), N // 2, ("h", i, k))
        o += KC * N // 2
    UB = R(o, 4 * UH).rearrange("p (c n) -> p c n", c=4)
    for c in range(4):
        areg(o + c * UH, UH, ("u", c))
    o += 4 * UH
    PB = R(o, 4 * NH).rearrange("p (c n) -> p c n", c=4)
    for c in range(4):
        areg(o + c * NH, NH, ("p", c))
    o += 4 * NH
    CV = R(o, 4 * N).rearrange("p (c n) -> p c n", c=4)
    for c in range(4):
        areg(o + c * N, N, ("cv", c))
    o += 4 * N
    SGA = R(o, 8 * N).rearrange("p (c n) -> p c n", c=8)
    for c in range(8):
        areg(o + c * N, N, ("sga", c))
    o += 8 * N
    SGB = R(o, 8 * N).rearrange("p (c n) -> p c n", c=8)
    for c in range(8):
        areg(o + c * N, N, ("sgb", c))
    o += 8 * N
    PT = []
    for i in range(2):
        PT.append(R(o, 4 * UH).rearrange("p (c n) -> p c n", c=4)); areg(o, 4 * UH, ("pt", i)); o += 4 * UH
    POOLED = RB(o, 4 * N // 2).rearrange("p (c n) -> p c n", c=4)
    for c in range(4):
        areg(o + c * (N // 2), N // 2, ("pooled", c))
    o += 4 * N // 2
    BGCV = RB(o, 4 * N // 2).rearrange("p (c n) -> p c n", c=4)
    for c in range(4):
        areg(o + c * (N // 2), N // 2, ("bgcv", c))
    o += 4 * N // 2
    MERGED = RB(o, 8 * N // 2).rearrange("p (c n) -> p c n", c=8)
    for c in range(8):
        areg(o + c * (N // 2), N // 2, ("merged", c))
    o += 8 * N // 2
    RBUF = R(o, 8 * N).rearrange("p (c n) -> p c n", c=8)
    for c in range(8):
        areg(o + c * N, N, ("r", c))
    o += 8 * N
    TMPF = R(o, 16); areg(o, 16, "tmpf"); o += 16
    STG = R(o, 196); areg(o, 196, "stg"); o += 196
    A_END = o
    PF_OFF = SBW - NPF * 512
    assert PF_OFF >= A_END, (PF_OFF, A_END)

    NWIN = 32
    o = WIN_OFF + 16384
    WUP_C_OFF = o
    o += (44 - NPF - NWIN) * 512
    WDN_OFF = o
    W_DN = RB(o, 11264).rearrange("p (j k c) -> p j k c", j=8, k=FC); o += 11264
    X1B = R(o, KC * N).rearrange("p (k n) -> p k n", k=KC); X1B_OFF = o; o += KC * N
    XSQB = RB(o, KC * N // 2).rearrange("p (k n) -> p k n", k=KC); XSQB_OFF = o; o += KC * N // 2
    YSQ = RB(o, KC * N // 2).rearrange("p (k n) -> p k n", k=KC); YSQ_OFF = o; o += KC * N // 2
    H2, H2_OFF = [], []
    for i in range(2):
        H2.append(RB(o, KC * NH // 2).rearrange("p (k n) -> p k n", k=KC)); H2_OFF.append(o); o += KC * NH // 2
    GB, GB_OFF = [], []
    for i in range(2):
        GB.append(RB(o, FC * N // 2).rearrange("p (c n) -> p c n", c=FC)); GB_OFF.append(o); o += FC * N // 2
    NSET = 4
    TG, TV, SG, TG_OFF, TV_OFF = [], [], [], [], []
    for i in range(2):
        TG.append(R(o, N)); TG_OFF.append(o); o += N
        TV.append(R(o, N)); TV_OFF.append(o); o += N
    for i in range(2):
        TG.append(R(256 + 2 * i * N, N)); TV.append(R(256 + (2 * i + 1) * N, N))
        TG_OFF.append(256 + 2 * i * N); TV_OFF.append(256 + (2 * i + 1) * N)
    assert 256 + 4 * N <= 1344
    for i in range(NSET):
        SG.append(TG[i])
    YB = R(o, 8 * N).rearrange("p (c n) -> p c n", c=8); YB_OFF = o; o += 8 * N
    TAIL = R(X1B_OFF, 44 * 18).rearrange("p (c n) -> p c n", c=44); TAIL_FLAT = R(X1B_OFF, 44 * 18)
    assert o <= PF_OFF, (o, PF_OFF)

    def wup_off(b):
        if b < NPF:
            return PF_OFF + 512 * b
        if b < NPF + NWIN:
            return WIN_OFF + 512 * (b - NPF)
        return WUP_C_OFF + 512 * (b - NPF - NWIN)

    def wup_blk(b):
        return RB(wup_off(b), 512).rearrange("p (k c) -> p k c", k=KC)

    def cc(col, n=1):
        return CST[:, col:col + n]

    def bc8(ap2d):
        return ap2d.unsqueeze(1).to_broadcast([128, 8, N])

    S = Sched()
    bank_rr = [0]

    def next_bank():
        b = bank_rr[0]
        bank_rr[0] = (b + 1) % 7
        return b

    SB = 7

    def load_x(ti):
        S.add("sp", lambda e: e.dma_start(out=XB[ti % 2].rearrange("p k n -> p (k n)"), in_=xT[ti]),
              writes=[("x", ti % 2, k) for k in range(KC)], dma=True)

    load_x(0)
    x0_load = S.ops[-1]
    S.add("sp", lambda e: e.dma_start(out=CST, in_=cst), writes=["cst"], dma=True)
    load_x(1)
    def load_states():
        S.add("sp", lambda e: e.dma_start(out=R(256, 960), in_=st_pool_d), writes=["st_pool"], dma=True)
        S.add("sp", lambda e: e.dma_start(out=R(1216, 128), in_=st_conv_d), writes=["st_conv"], dma=True)
        S.add("sp", lambda e: e.dma_start(out=R(1344, 1408), in_=st_ffn_d), writes=["st_ffn"], dma=True)
        S.add("sp", lambda e: e.dma_start(out=o_pool_sa, in_=ST_POOL[:, :, 1:15, :].rearrange("p c r s -> p c (r s)")),
              reads=["st_pool"], writes=["o3"], dma=True)
        S.add("sp", lambda e: e.dma_start(out=o_conv_st, in_=R(1216, 128)), reads=["st_conv"], writes=["o5"], dma=True)
        S.add("sp", lambda e: e.dma_start(out=o_ffn_st, in_=R(1344, 1408)), reads=["st_ffn"], writes=["o7"], dma=True)


    def wload(dst_off, src, src_off_elems, n_elems, toks, nb=False, al=False, extra=()):
        dst = RB(dst_off, n_elems // 2)
        extra_w = alias(dst_off, n_elems // 2) if al else []
        S.add("pool", lambda e: e.dma_start(out=dst, in_=src[:, src_off_elems:src_off_elems + n_elems],
                                            max_dma_last_dim=8192),
              writes=list(toks), after=extra_w, dma=True, nb=nb, extra=extra)

    BL = KC * 128
    S.add("pool", lambda e: e.memset(ONES, 1.0), writes=["ones"])
    S.add("pool", lambda e: e.memset(UB[:, :, 0:15], 0.0), writes=[("u", c) for c in range(4)])
    S.add("pool", lambda e: e.memset(PB[:, :, 0:2], 0.0), writes=[("p", c) for c in range(4)])
    for q in range(8):
        wload(WIN_OFF + q * 4 * BL // 2, w_in, q * 4 * BL, 4 * BL, [("w_in", j) for j in range(4 * q, 4 * q + 4)],
              extra=([x0_load] if q == 1 else ()))
    wload(WPOOL_OFF, w_pool, 0, 1024, ["w_pool"])
    wload(WCO_OFF, w_co, 0, 4096, ["w_co"])
    for q in range(2):
        wload(WO_OFF + q * 2048, w_o, q * 4096, 4096, [("w_o", j) for j in range(4 * q, 4 * q + 4)])
    b0 = 0
    while b0 < NPF:
        nb = min(4, NPF - b0)
        wload(wup_off(b0), w_up, b0 * BL, nb * BL, [("w_up", b0 + i) for i in range(nb)])
        b0 += nb

    EPSB = CST[:, 255:256]
    S.add("dve", lambda e: e.memset(EPSB, EPS), reads=["cst"], writes=["eps"])
    S.add("act", lambda e: e.activation(out=TMPF[:, 0:2], in_=ONES[:, 0:2], func=AF.Sqrt),
          reads=["ones"], writes=["tmpf"])

    def norm_stats(sq, sq_tok, sd_i):
        def mm(e):
            for k in range(KC):
                ins = e.matmul(bank(SB), lhsT=ONES, rhs=sq[:, k, :], start=(k == 0), stop=(k == KC - 1))
            return ins
        S.add("pe", mm, reads=["ones"] + [(sq_tok, k) for k in range(KC)], writes=[("ps", SB)])
        S.add("act", lambda e: e.activation(out=SD[sd_i], in_=bank(SB), func=AF.Sqrt, scale=1.0 / D, bias=EPSB),
              reads=[("ps", SB), "eps"], writes=[("sd", sd_i)])
        S.add("dve", lambda e: e.reciprocal(out=SD[sd_i], in_=SD[sd_i]),
              reads=[("sd", sd_i)], writes=[("sd", sd_i)])

    def prenorm_A(ti):
        xb = XB[ti % 2]
        hb = HB[ti % 2]
        S.add("act", lambda e: e.activation(out=XSQ, in_=xb, func=AF.Square),
              reads=[("x", ti % 2, k) for k in range(KC)], writes=[("xsq", k) for k in range(KC)])
        norm_stats(XSQ, "xsq", 0)
        for k in range(KC):
            S.add("dve", lambda e, k=k: e.scalar_tensor_tensor(
                out=hb[:, k, :], in0=xb[:, k, :], scalar=cc(C_G1 + k), in1=SD[0], op0=ALU.mult, op1=ALU.mult),
                reads=[("x", ti % 2, k), ("sd", 0), "cst"], writes=[("h", ti % 2, k)])

    def win_group(ti, jj):
        hb = HB[ti % 2]
        b = next_bank()

        def mm(e):
            for k in range(KC):
                ins = e.matmul(bank(b), lhsT=W_IN[:, jj, k, :], rhs=hb[:, k, :], start=(k == 0), stop=(k == KC - 1))
            return ins
        S.add("pe", mm, reads=[("w_in", jj)] + [("h", ti % 2, k) for k in range(KC)], writes=[("ps", b)])
        return b

    def A_part1(ti):
        for c in range(4):
            b = win_group(ti, c)
            S.add("act", lambda e, b=b, c=c: e.activation(out=PB[:, c, 2:NH], in_=bank(b), func=AF.Copy),
                  reads=[("ps", b)], writes=[("p", c)])
        for c in range(4):
            b = win_group(ti, 4 + c)
            S.add("dve", lambda e, b=b, c=c: e.tensor_tensor(out=PB[:, c, 2:NH], in0=bank(b), in1=PB[:, c, 2:NH],
                                                             op=ALU.mult),
                  reads=[("ps", b), ("p", c)], writes=[("p", c)])
        for c in range(4):
            b = win_group(ti, 8 + c)
            S.add("act", lambda e, b=b, c=c: e.activation(out=UB[:, c, 15:UH], in_=bank(b), func=AF.Copy),
                  reads=[("ps", b)], writes=[("u", c)])

    def A_epilogue(ti):
        last = (ti == NT - 1)
        xb = XB[ti % 2]
        norm_stats(XSQ, "xsq", 1)
        S.add("dve", lambda e: e.tensor_tensor(out=RBUF, in0=RBUF, in1=bc8(SD[1]), op=ALU.mult),
              reads=[("r", m) for m in range(8)] + [("sd", 1)], writes=[("r", m) for m in range(8)])
        S.add("dve", lambda e: e.tensor_tensor(out=xb, in0=xb, in1=RBUF, op=ALU.add),
              reads=[("r", m) for m in range(8)] + [("x", ti % 2, m) for m in range(8)],
              writes=[("x", ti % 2, m) for m in range(8)])
        S.add("sp", lambda e: e.dma_start(out=x1T[ti], in_=xb.rearrange("p k n -> p (k n)")),
              reads=[("x", ti % 2, k) for k in range(KC)], writes=[("x1T", ti)], dma=True)
        if ti + 2 < NT:
            load_x(ti + 2)

    def A_part2(ti):
        last = (ti == NT - 1)
        if last:
            for c, w in enumerate(WINS):
                if w == 2:
                    S.add("dve", lambda e, c=c: e.tensor_copy(out=RED[:, c, :], in_=ST_POOL[:, c, 14, :]),
                          reads=["st_pool"], writes=[("red", c)])
                else:
                    S.add("dve", lambda e, c=c, w=w: e.tensor_reduce(
                        out=RED[:, c, :], in_=ST_POOL[:, c, 16 - w:15, :].rearrange("p r s -> p s r"),
                        axis=AX.X, op=ALU.add), reads=["st_pool"], writes=[("red", c)])
        for c in range(4):
            S.add("act", lambda e, c=c: e.activation(out=CV[:, c, :], in_=PB[:, c, 0:N], func=AF.Copy,
                                                     scale=cc(C_WC + c)),
                  reads=[("p", c), "cst"], writes=[("cv", c)])
            S.add("dve", lambda e, c=c: e.scalar_tensor_tensor(out=CV[:, c, :], in0=PB[:, c, 1:N + 1],
                                                               scalar=cc(C_WC + 4 + c), in1=CV[:, c, :],
                                                               op0=ALU.mult, op1=ALU.add),
                  reads=[("p", c), ("cv", c), "cst"], writes=[("cv", c)])
            S.add("dve", lambda e, c=c: e.scalar_tensor_tensor(out=CV[:, c, :], in0=PB[:, c, 2:N + 2],
                                                               scalar=cc(C_WC + 8 + c), in1=CV[:, c, :],
                                                               op0=ALU.mult, op1=ALU.add),
                  reads=[("p", c), ("cv", c), "cst"], writes=[("cv", c)])
            if last:
                S.add("dve", lambda e, c=c: e.tensor_scalar(out=CV[:, c, SC:N], in0=ST_CONV[:, c, 0, :],
                                                            scalar1=cc(C_WC + c), scalar2=None, op0=ALU.mult),
                      reads=["st_conv", "cst", ("cv", c)], writes=[("cv", c)])
                S.add("dve", lambda e, c=c: e.scalar_tensor_tensor(out=CV[:, c, SC:N], in0=ST_CONV[:, c, 1, :],
                                                                   scalar=cc(C_WC + 4 + c), in1=CV[:, c, SC:N],
                                                                   op0=ALU.mult, op1=ALU.add),
                      reads=["st_conv", "cst", ("cv", c)], writes=[("cv", c)])
                S.add("dve", lambda e, c=c: e.scalar_tensor_tensor(out=CV[:, c, SC:N], in0=PB[:, c, 2 + SC:NH],
                                                                   scalar=cc(C_WC + 8 + c), in1=CV[:, c, SC:N],
                                                                   op0=ALU.mult, op1=ALU.add),
                      reads=[("p", c), "cst", ("cv", c)], writes=[("cv", c)])
        for c in range(8):
            b = win_group(ti, 12 + c)
            S.add("act", lambda e, b=b, c=c: e.activation(out=SGA[:, c, :], in_=bank(b), func=AF.Sigmoid),
                  reads=[("ps", b)], writes=[("sga", c)])
        utoks = [("u", c) for c in range(4)]
        S.add("dve", lambda e: e.tensor_tensor(out=PT[0][:, 0:4, 1:UH], in0=UB[:, 0:4, 1:UH], in1=UB[:, 0:4, 0:UH - 1],
                                               op=ALU.add), reads=utoks, writes=[("pt", 0)])
        S.add("dve", lambda e: e.tensor_tensor(out=PT[1][:, 1:4, 3:UH], in0=PT[0][:, 1:4, 3:UH],
                                               in1=PT[0][:, 1:4, 1:UH - 2], op=ALU.add),
              reads=[("pt", 0)], writes=[("pt", 1)])
        S.add("dve", lambda e: e.tensor_tensor(out=PT[0][:, 2:4, 7:UH], in0=PT[1][:, 2:4, 7:UH],
                                               in1=PT[1][:, 2:4, 3:UH - 4], op=ALU.add),
              reads=[("pt", 1), ("pt", 0)], writes=[("pt", 0)])
        S.add("dve", lambda e: e.tensor_tensor(out=PT[1][:, 3, 15:UH], in0=PT[0][:, 3, 15:UH],
                                               in1=PT[0][:, 3, 7:UH - 8], op=ALU.add),
              reads=[("pt", 0), ("pt", 1)], writes=[("pt", 1)])
        for c, w in enumerate(WINS):
            cur = PT[c % 2][:, c, :]
            ptoks = [("pt", 0), ("pt", 1)]
            S.add("dve", lambda e, c=c, w=w, cur=cur: e.scalar_tensor_tensor(
                out=POOLED[:, c, :], in0=cur[:, 15:UH], scalar=1.0 / w, in1=UB[:, c, 15:UH],
                op0=ALU.mult, op1=ALU.subtract),
                reads=ptoks + [("u", c)], writes=[("pooled", c)])
            if ti == 0:
                S.add("dve", lambda e, c=c, cur=cur: e.tensor_tensor(out=TMPF, in0=cur[:, 15:31],
                                                                     in1=CST[:, C_IC + 16 * c:C_IC + 16 * c + 16],
                                                                     op=ALU.mult),
                      reads=ptoks + ["cst"], writes=["tmpf"])
                S.add("dve", lambda e, c=c: e.tensor_tensor(out=POOLED[:, c, 0:16], in0=TMPF, in1=UB[:, c, 15:31],
                                                            op=ALU.subtract),
                      reads=["tmpf", ("u", c), ("pooled", c)], writes=[("pooled", c)])
            if last:
                S.add("dve", lambda e, c=c: e.tensor_tensor(out=TMPF, in0=RED[:, c, :], in1=UB[:, c, 15 + SC:UH],
                                                            op=ALU.add),
                      reads=[("red", c), ("u", c)], writes=["tmpf"])
                S.add("dve", lambda e, c=c, w=w: e.scalar_tensor_tensor(
                    out=POOLED[:, c, SC:N], in0=TMPF, scalar=1.0 / w, in1=UB[:, c, 15 + SC:UH],
                    op0=ALU.mult, op1=ALU.subtract),
                    reads=["tmpf", ("u", c), ("pooled", c)], writes=[("pooled", c)])
        for c in range(8):
            b = win_group(ti, 20 + c)
            S.add("act", lambda e, b=b, c=c: e.activation(out=SGB[:, c, :], in_=bank(b), func=AF.Sigmoid),
                  reads=[("ps", b)], writes=[("sgb", c)])
        for c in range(4):
            b = win_group(ti, 28 + c)
            S.add("dve", lambda e, b=b, c=c: e.tensor_tensor(out=BGCV[:, c, :], in0=bank(b), in1=CV[:, c, :],
                                                             op=ALU.mult),
                  reads=[("ps", b), ("cv", c)], writes=[("bgcv", c)])
        if last:
            early_w_up(NPF, NPF + 12)
        if not last:
            S.add("pool", lambda e: e.tensor_copy(out=UB[:, :, 0:15], in_=UB[:, :, N:N + 15]),
                  reads=[("u", c) for c in range(4)], writes=[("u", c) for c in range(4)])
            S.add("pool", lambda e: e.tensor_copy(out=PB[:, :, 0:2], in_=PB[:, :, N:N + 2]),
                  reads=[("p", c) for c in range(4)], writes=[("p", c) for c in range(4)])
        else:
            S.add("pool", lambda e: e.tensor_copy(out=STG[:, 0:60].rearrange("p (c n) -> p c n", c=4), in_=UB[:, :, 244:259]),
                  reads=[("u", c) for c in range(4)], writes=["stg"])
            S.add("pool", lambda e: e.tensor_copy(out=STG[:, 60:68].rearrange("p (c n) -> p c n", c=4), in_=PB[:, :, 244:246]),
                  reads=[("p", c) for c in range(4)], writes=["stg"])
            S.add("pool", lambda e: e.tensor_copy(out=STG[:, 68:132].rearrange("p (c n) -> p c n", c=4), in_=UB[:, :, 15 + SC:UH]),
                  reads=[("u", c) for c in range(4)], writes=["stg"])
            S.add("pool", lambda e: e.tensor_copy(out=STG[:, 132:196].rearrange("p (c n) -> p c n", c=4), in_=PB[:, :, 2 + SC:NH]),
                  reads=[("p", c) for c in range(4)], writes=["stg"])
            S.add("sp", lambda e: e.dma_start(out=o_small, in_=STG), reads=["stg"], writes=["o1"], dma=True)
        for m in range(8):
            b = next_bank()
            S.add("pe", lambda e, b=b, m=m: e.matmul(bank(b), lhsT=W_POOL[:, m, :], rhs=POOLED[:, m // 2, :],
                                                     start=True, stop=True),
                  reads=["w_pool", ("pooled", m // 2)], writes=[("ps", b)])
            S.add("dve", lambda e, b=b, m=m: e.scalar_tensor_tensor(out=SGA[:, m, :], in0=bank(b),
                                                                    scalar=cc(C_PS + m), in1=SGA[:, m, :],
                                                                    op0=ALU.mult, op1=ALU.mult),
                  reads=[("ps", b), ("sga", m), "cst"], writes=[("sga", m)])
        for m in range(8):
            b = next_bank()

            def mm(e, b=b, m=m):
                for k in range(4):
                    ins = e.matmul(bank(b), lhsT=W_CO[:, m, k, :], rhs=BGCV[:, k, :], start=(k == 0), stop=(k == 3))
                return ins
            S.add("pe", mm, reads=["w_co"] + [("bgcv", k) for k in range(4)], writes=[("ps", b)])
            S.add("dve", lambda e, b=b, m=m: e.tensor_tensor(out=SGB[:, m, :], in0=bank(b), in1=SGB[:, m, :],
                                                             op=ALU.mult),
                  reads=[("ps", b), ("sgb", m)], writes=[("sgb", m)])
            if m % 4 == 3:
                m0 = m - 3
                S.add("dve", lambda e, m0=m0: e.tensor_tensor(out=MERGED[:, m0:m0 + 4, :], in0=SGA[:, m0:m0 + 4, :],
                                                              in1=SGB[:, m0:m0 + 4, :], op=ALU.add),
                      reads=[("sga", q) for q in range(m0, m0 + 4)] + [("sgb", q) for q in range(m0, m0 + 4)],
                      writes=[("merged", q) for q in range(m0, m0 + 4)])
        if ti + 1 < NT:
            prenorm_A(ti + 1)
        else:
            start_ffn_phase()
        NSPLIT = 6
        HK = KC // 2
        wbanks = [next_bank() for _ in range(NSPLIT)]

        def r_evac(bk, m):
            S.add("act", lambda e: e.activation(out=RBUF[:, m, :], in_=bank(bk), func=AF.Copy, scale=cc(C_G2 + m)),
                  reads=[("ps", bk), "cst"], writes=[("r", m)])
            S.add("act", lambda e: e.activation(out=XSQ[:, m, :], in_=bank(bk), func=AF.Square),
                  reads=[("ps", bk)], writes=[("xsq", m)])

        for m in range(NSPLIT):
            def mm1(e, bk=wbanks[m], m=m):
                for k in range(HK):
                    ins = e.matmul(bank(bk), lhsT=W_O[:, m, k, :], rhs=MERGED[:, k, :], start=(k == 0), stop=False)
                return ins
            S.add("pe", mm1, reads=[("w_o", m)] + [("merged", k) for k in range(HK)], writes=[("ps", wbanks[m])])
        for m in range(NSPLIT):
            def mm2(e, bk=wbanks[m], m=m):
                for k in range(HK, KC):
                    ins = e.matmul(bank(bk), lhsT=W_O[:, m, k, :], rhs=MERGED[:, k, :], start=False, stop=(k == KC - 1))
                return ins
            S.add("pe", mm2, reads=[("w_o", m)] + [("merged", k) for k in range(HK, KC)], writes=[("ps", wbanks[m])])
            r_evac(wbanks[m], m)
        for m in range(NSPLIT, 8):
            bk = next_bank()

            def mm(e, bk=bk, m=m):
                for k in range(KC):
                    ins = e.matmul(bank(bk), lhsT=W_O[:, m, k, :], rhs=MERGED[:, k, :], start=(k == 0), stop=(k == KC - 1))
                return ins
            S.add("pe", mm, reads=[("w_o", m)] + [("merged", k) for k in range(KC)], writes=[("ps", bk)])
            r_evac(bk, m)

    def load_x1(ti):
        S.add("sp", lambda e: e.dma_start(out=X1B.rearrange("p k n -> p (k n)"), in_=x1T[ti]),
              reads=[("x1T", ti)], writes=[("x1", k) for k in range(KC)], after=alias(X1B_OFF, KC * N), dma=True)

    def y_init(ti):
        S.add("sp", lambda e: e.dma_start(out=yT[ti], in_=x1T[ti]),
              reads=[("x1T", ti)], writes=[("yT", ti)], dma=True)

    def early_w_up(b_lo, b_hi, extra=()):
        b0 = b_lo
        while b0 < b_hi:
            dst = RB(wup_off(b0), 4 * BL // 2)
            S.add("pool", lambda e, dst=dst, b0=b0: e.dma_start(out=dst, in_=w_up[:, b0 * BL:(b0 + 4) * BL],
                                                                max_dma_last_dim=8192),
                  writes=[("w_up", b0 + i) for i in range(4)] + [("w_in", b0 - NPF + i) for i in range(4)],
                  dma=True, extra=extra)
            b0 += 4

    def start_ffn_phase():
        load_x1(0)
        x1_load = S.ops[-1]
        S.add("pool", lambda e: e.memset(H2[0][:, :, 0:2], 0.0),
              writes=[("h2", 0, k) for k in range(KC)], after=alias(H2_OFF[0], KC * NH // 2))
        early_w_up(NPF + 12, NPF + NWIN, extra=[x1_load])
        for st in range(16):
            prenormB_piece(0, st)

    def prenorm_B(ti):
        hb = H2[ti % 2]
        S.add("act", lambda e: e.activation(out=XSQB, in_=X1B, func=AF.Square),
              reads=[("x1", k) for k in range(KC)], writes=[("xsqb", k) for k in range(KC)])
        norm_stats(XSQB, "xsqb", 0)
        if ti > 0:
            S.add("pool", lambda e: e.tensor_copy(out=hb[:, :, 0:2], in_=H2[(ti - 1) % 2][:, :, N:NH]),
                  reads=[("h2", (ti - 1) % 2, k) for k in range(KC)], writes=[("h2", ti % 2, k) for k in range(KC)])
        for k in range(KC):
            S.add("dve", lambda e, k=k: e.scalar_tensor_tensor(
                out=hb[:, k, 2:NH], in0=X1B[:, k, :], scalar=cc(C_G3 + k), in1=SD[0], op0=ALU.mult, op1=ALU.mult),
                reads=[("x1", k), ("sd", 0), "cst"], writes=[("h2", ti % 2, k)])
        if ti + 1 < NT:
            load_x1(ti + 1)

    HN = N // 2

    def prenormB_piece(ti, step):
        hb = H2[ti % 2]
        if step in (0, 1):
            c0 = 4 * step
            S.add("act", lambda e: e.activation(out=XSQB[:, c0:c0 + 4, :], in_=X1B[:, c0:c0 + 4, :], func=AF.Square),
                  reads=[("x1", k) for k in range(c0, c0 + 4)],
                  writes=[("xsqb", k) for k in range(c0, c0 + 4)], after=alias(XSQB_OFF + c0 * (N // 2), 4 * (N // 2)))
        elif step == 2:
            def mm(e):
                for k in range(KC):
                    ins = e.matmul(bank(SB), lhsT=ONES, rhs=XSQB[:, k, :], start=(k == 0), stop=(k == KC - 1))
                return ins
            S.add("pe", mm, reads=["ones"] + [("xsqb", k) for k in range(KC)], writes=[("ps", SB)])
            S.add("act", lambda e: e.activation(out=SD[0], in_=bank(SB), func=AF.Sqrt, scale=1.0 / D, bias=EPSB),
                  reads=[("ps", SB), "eps"], writes=[("sd", 0)])
            if ti > 0:
                S.add("pool", lambda e: e.tensor_copy(out=hb[:, :, 0:2], in_=H2[(ti - 1) % 2][:, :, N:NH]),
                      reads=[("h2", (ti - 1) % 2, k) for k in range(KC)],
                      writes=[("h2", ti % 2, k) for k in range(KC)], after=alias(H2_OFF[ti % 2], KC * NH // 2))
        elif step in (3, 4):
            h = step - 3
            S.add("dve", lambda e: e.reciprocal(out=SD[0][:, h * HN:(h + 1) * HN], in_=SD[0][:, h * HN:(h + 1) * HN]),
                  reads=[("sd", 0)], writes=[("sd", 0)])
        elif 5 <= step < 13:
            k = step - 5
            S.add("act", lambda e: e.activation(out=X1B[:, k, :], in_=X1B[:, k, :], func=AF.Copy, scale=cc(C_G3 + k)),
                  reads=[("x1", k), "cst"], writes=[("x1", k)])
        elif step in (13, 14):
            c0 = 4 * (step - 13)
            S.add("dve", lambda e: e.tensor_tensor(out=hb[:, c0:c0 + 4, 2:NH], in0=X1B[:, c0:c0 + 4, :],
                                                   in1=SD[0].unsqueeze(1).to_broadcast([128, 4, N]), op=ALU.mult),
                  reads=[("x1", k) for k in range(c0, c0 + 4)] + [("sd", 0)],
                  writes=[("h2", ti % 2, k) for k in range(c0, c0 + 4)])
        elif step == 15:
            if ti + 1 < NT:
                load_x1(ti + 1)

    def epilogueB_piece(ti, step):
        if step == 0:
            def mm(e):
                for k in range(KC):
                    ins = e.matmul(bank(SB), lhsT=ONES, rhs=YSQ[:, k, :], start=(k == 0), stop=(k == KC - 1))
                return ins
            S.add("pe", mm, reads=["ones"] + [("ysq", k) for k in range(KC)], writes=[("ps", SB)])
            S.add("act", lambda e: e.activation(out=SD[1], in_=bank(SB), func=AF.Sqrt, scale=1.0 / D, bias=EPSB),
                  reads=[("ps", SB), "eps"], writes=[("sd", 1)])
        elif step in (1, 2):
            h = step - 1
            S.add("dve", lambda e: e.reciprocal(out=SD[1][:, h * HN:(h + 1) * HN], in_=SD[1][:, h * HN:(h + 1) * HN]),
                  reads=[("sd", 1)], writes=[("sd", 1)])
        elif step in (3, 4):
            c0 = 4 * (step - 3)
            S.add("dve", lambda e: e.tensor_tensor(out=YB[:, c0:c0 + 4, :], in0=YB[:, c0:c0 + 4, :],
                                                   in1=SD[1].unsqueeze(1).to_broadcast([128, 4, N]), op=ALU.mult),
                  reads=[("y", m) for m in range(c0, c0 + 4)] + [("sd", 1)],
                  writes=[("y", m) for m in range(c0, c0 + 4)])
            if ti == NT - 1:
                S.add("pool", lambda e: e.dma_start(out=yT[ti][:, c0 * N:(c0 + 4) * N],
                                                    in_=YB[:, c0:c0 + 4, :].rearrange("p c n -> p (c n)"),
                                                    accum_op=ALU.add, max_dma_last_dim=4096),
                      reads=[("y", m) for m in range(c0, c0 + 4)] + [("yT", ti)], writes=[("yT", ti, c0)], dma=True)
        elif step == 5 and ti == NT - 1:
            S.add("sp", lambda e: e.dma_start(out=o_ffn_tail, in_=TAIL_FLAT),
                  reads=[("tail", j) for j in range(44)], writes=["o8"], dma=True)
        elif step == 5:
            S.add("pool", lambda e: e.dma_start(out=yT[ti], in_=YB.rearrange("p c n -> p (c n)"), accum_op=ALU.add,
                                                max_dma_last_dim=4096),
                  reads=[("y", m) for m in range(8)] + [("yT", ti)], writes=[("yT", ti)], dma=True)
            if ti == NT - 1:
                S.add("sp", lambda e: e.dma_start(out=o_ffn_tail, in_=TAIL_FLAT),
                      reads=[("tail", j) for j in range(44)], writes=["o8"], dma=True)

    set_rr = [0]

    pending = []
    pending_m = []

    def flush_silu():
        while pending:
            pending.pop(0)()

    def flush_mult():
        while pending_m:
            pending_m.pop(0)()

    def flush_pending():
        flush_silu()
        flush_mult()

    def B_pair(ti, j):
        last = (ti == NT - 1)
        hb = H2[ti % 2]
        gb = GB[ti % 2]
        si = set_rr[0]
        set_rr[0] = (si + 1) % NSET
        halves = []
        for half, (blk, fj, T) in enumerate(((2 * j, j, TG[si]), (2 * j + 1, FC + j, TV[si]))):
            bk = next_bank()
            wb = wup_blk(blk)

            def mm(e, bk=bk, wb=wb):
                for k in range(KC):
                    ins = e.matmul(bank(bk, NH), lhsT=wb[:, k, :], rhs=hb[:, k, :], start=(k == 0), stop=(k == KC - 1))
                return ins
            S.add("pe", mm, reads=[("w_up", blk)] + [("h2", ti % 2, k) for k in range(KC)], writes=[("ps", bk)])
            halves.append((bk, fj, T, ("t", si, half), (TG_OFF[si], TV_OFF[si])[half]))
        for bk, fj, T, ttok, toff in halves:
            S.add("act", lambda e, bk=bk, T=T, fj=fj: e.activation(out=T, in_=ps[:, bk * 512:bk * 512 + N], func=AF.Copy,
                                                                    scale=cc(C_WF + fj)),
                  reads=[("ps", bk), "cst"], writes=[ttok], after=alias(toff, N))
        flush_silu()
        for bk, fj, T, ttok, toff in halves:
            S.add("dve", lambda e, bk=bk, T=T, fj=fj: e.scalar_tensor_tensor(
                out=T, in0=ps[:, bk * 512 + 1:bk * 512 + 1 + N], scalar=cc(C_WF + 44 + fj), in1=T,
                op0=ALU.mult, op1=ALU.add),
                reads=[("ps", bk), ttok, "cst"], writes=[ttok])
        for bk, fj, T, ttok, toff in halves:
            S.add("dve", lambda e, bk=bk, T=T, fj=fj: e.scalar_tensor_tensor(
                out=T, in0=ps[:, bk * 512 + 2:bk * 512 + 2 + N], scalar=cc(C_WF + 88 + fj), in1=T,
                op0=ALU.mult, op1=ALU.add),
                reads=[("ps", bk), ttok, "cst"], writes=[ttok])
        if last:
            for bk, fj, T, ttok, toff in halves:
                S.add("dve", lambda e, bk=bk, T=T, fj=fj: e.scalar_tensor_tensor(
                    out=T[:, SC:N], in0=ps[:, bk * 512 + 2 + SC:bk * 512 + NH], scalar=cc(C_WF + 88 + fj),
                    in1=PRE[:, fj, :], op0=ALU.mult, op1=ALU.add),
                    reads=[("ps", bk), ttok, "cst", "pre"], writes=[ttok])
                pending.append(lambda bk=bk, fj=fj, ttok=ttok: S.add(
                    "act", lambda e: e.activation(out=TAIL[:, fj, :], in_=ps[:, bk * 512 + SC:bk * 512 + NH], func=AF.Copy),
                    reads=[("ps", bk), ttok], writes=[("tail", fj)] + [("x1", k) for k in range(KC)]))
        flush_mult()
        gtok, vtok = halves[0][3], halves[1][3]
        pending.append(lambda si=si: S.add("act", lambda e: e.activation(out=SG[si], in_=TG[si], func=AF.Silu),
                                           reads=[gtok], writes=[gtok]))
        pending_m.append(lambda si=si, j=j: S.add(
            "dve", lambda e: e.tensor_tensor(out=gb[:, j, :], in0=SG[si], in1=TV[si], op=ALU.mult),
            reads=[gtok, vtok], writes=[("g", ti % 2, j)], after=alias(GB_OFF[ti % 2] + j * (N // 2), N // 2)))

    def B_wdown_group(ti, m):
        gb = GB[ti % 2]
        b = next_bank()

        def mm(e):
            for k in range(FC):
                ins = e.matmul(bank(b), lhsT=W_DN[:, m, k, :], rhs=gb[:, k, :], start=(k == 0), stop=(k == FC - 1))
            return ins
        S.add("pe", mm, reads=[("w_dn", m)] + [("g", ti % 2, k) for k in range(FC)], writes=[("ps", b)])
        S.add("act", lambda e: e.activation(out=YB[:, m, :], in_=bank(b), func=AF.Copy, scale=cc(C_G4 + m)),
              reads=[("ps", b), "cst"], writes=[("y", m)], after=alias(YB_OFF + m * N, N))
        S.add("act", lambda e: e.activation(out=YSQ[:, m, :], in_=bank(b), func=AF.Square),
              reads=[("ps", b)], writes=[("ysq", m)], after=alias(YSQ_OFF + m * (N // 2), N // 2))

    def B_epilogue(ti):
        last = (ti == NT - 1)
        norm_stats(YSQ, "ysq", 1)
        S.add("dve", lambda e: e.tensor_tensor(out=YB, in0=YB, in1=bc8(SD[1]), op=ALU.mult),
              reads=[("y", m) for m in range(8)] + [("sd", 1)], writes=[("y", m) for m in range(8)])
        S.add("pool", lambda e: e.dma_start(out=yT[ti], in_=YB.rearrange("p c n -> p (c n)"), accum_op=ALU.add,
                                            max_dma_last_dim=4096),
              reads=[("y", m) for m in range(8)] + [("yT", ti)], writes=[("yT", ti)], dma=True)
        if last:
            S.add("sp", lambda e: e.dma_start(out=o_ffn_tail, in_=TAIL_FLAT),
                  reads=[("tail", j) for j in range(44)], writes=["o8"], dma=True)

    def pre_ffn_state():
        w0b = cc(C_WF, 44).unsqueeze(2).to_broadcast([128, 44, NS])
        w1b = cc(C_WF + 44, 44).unsqueeze(2).to_broadcast([128, 44, NS])
        S.add("dve", lambda e: e.tensor_tensor(out=PRE, in0=ST_FFN[:, :, 0, :], in1=w0b, op=ALU.mult),
              reads=["st_ffn", "cst"], writes=["pre"])
        S.add("dve", lambda e: e.tensor_tensor(out=ST_FFN[:, :, 0, :], in0=ST_FFN[:, :, 1, :], in1=w1b, op=ALU.mult),
              reads=["st_ffn", "cst"], writes=["st_ffn0", "st_ffn"])
        S.add("dve", lambda e: e.tensor_tensor(out=PRE, in0=PRE, in1=ST_FFN[:, :, 0, :], op=ALU.add),
              reads=["pre", "st_ffn0"], writes=["pre"])

    WD_AFTER = {8: 0, 10: 1, 12: 2, 14: 3, 16: 4, 18: 5, 20: 6, 21: 7}
    PRE_AT = {2: [0], 3: [1], 4: [2], 5: [3, 5], 6: [4, 6], 13: [13], 14: [14], 16: [15]}
    for k in range(2, KC):
        PRE_AT[5 + k] = [5 + k]
    EPI_AT = {}
    EPI_NEXT = {0: 0, 1: 1, 2: 2, 3: 3, 4: 4, 5: 5}

    prenorm_A(0)
    for ti in range(NT):
        A_part1(ti)
        if ti > 0:
            A_epilogue(ti - 1)
        A_part2(ti)
        if ti == 1:
            load_states()
    A_epilogue(NT - 1)
    def late_weight_loads():
        b0 = NPF + NWIN
        while b0 < 44:
            nb = min(4, 44 - b0)
            wload(wup_off(b0), w_up, b0 * BL, nb * BL, [("w_up", b0 + i) for i in range(nb)], al=True)
            b0 += nb
        for m in range(8):
            wload(WDN_OFF + m * 1408, w_dn, m * FC * 128, FC * 128, [("w_dn", m)], al=True)
    for ti in range(NT):
        for j in range(FC):
            if j == 0:
                flush_pending()
            B_pair(ti, j)
            if ti > 0 and j in WD_AFTER:
                B_wdown_group(ti - 1, WD_AFTER[j])
            if ti > 0 and j in EPI_AT:
                epilogueB_piece(ti - 1, EPI_AT[j])
            if ti > 1 and j in EPI_NEXT:
                epilogueB_piece(ti - 2, EPI_NEXT[j])
            if ti + 1 < NT and j in PRE_AT:
                for st in PRE_AT[j]:
                    prenormB_piece(ti + 1, st)
            if j == 3:
                y_init(ti)
            if j == 6 and ti == 0:
                late_weight_loads()
            if j == 5 and ti == NT - 2:
                pre_ffn_state()
    flush_pending()
    for st in range(6):
        epilogueB_piece(NT - 2, st)
    for m in range(8):
        B_wdown_group(NT - 1, m)
    for st in range(6):
        epilogueB_piece(NT - 1, st)

    S.add("sp", None, extra=list(S.dma_ops))

    S.finalize(nc, stack)
    with nc.Block() as block:
        @block.tensor
        def _(e):
            S.emit("pe", e)

        @block.scalar
        def _(e):
            S.emit("act", e)

        @block.vector
        def _(e):
            S.emit("dve", e)

        @block.gpsimd
        def _(e):
            S.emit("pool", e)

        @block.sync
        def _(e):
            S.emit("sp", e)
    stack.close()
    return nc


def _blk(w, nk):
    K, C = w.shape
    return np.ascontiguousarray(w.reshape(nk, 128, C // 128, 128).transpose(1, 2, 0, 3))


def _vec(v):
    return v.reshape(-1, 128).T


_NC_CACHE = {}


def kernel(x_prompt, x_sample, state_pool, state_conv, state_ffn, meta_tokens, w_in, w_pool,
           pool_scale, w_conv, w_conv_out, w_o, w_up, w_ffn_conv, w_down, g_pre_mix,
           g_post_mix, g_pre_ffn, g_post_ffn):
    f = np.float32
    x_prompt = np.asarray(x_prompt, f); x_sample = np.asarray(x_sample, f)
    state_pool = np.asarray(state_pool, f); state_conv = np.asarray(state_conv, f)
    state_ffn = np.asarray(state_ffn, f); meta_tokens = np.asarray(meta_tokens, f)
    B = x_prompt.shape[0]

    win = np.asarray(w_in, f)[0]
    win_b = _blk(win, KC)[:, WIN_ORDER]
    wpool = np.asarray(w_pool, f)[0]
    wpool_b = wpool.transpose(1, 0, 2).reshape(128, 4, 2, 128)
    wco_b = _blk(np.asarray(w_conv_out, f)[0], 4)
    wo_b = _blk(np.asarray(w_o, f)[0], KC)
    wup_b = _blk(np.asarray(w_up, f)[0], KC)
    order = []
    for j in range(FC):
        order += [j, FC + j]
    wup_b = wup_b[:, order]
    wdn_b = _blk(np.asarray(w_down, f)[0], FC)
    cst = np.zeros((128, 256), f)
    cst[:, C_G1:C_G1 + 8] = _vec(np.asarray(g_pre_mix, f)[0])
    cst[:, C_G2:C_G2 + 8] = _vec(np.asarray(g_post_mix, f)[0])
    cst[:, C_G3:C_G3 + 8] = _vec(np.asarray(g_pre_ffn, f)[0])
    cst[:, C_G4:C_G4 + 8] = _vec(np.asarray(g_post_ffn, f)[0])
    cst[:, C_PS:C_PS + 8] = _vec(np.asarray(pool_scale, f)[0])
    wc = np.asarray(w_conv, f)[0]
    cst[:, C_WC:C_WC + 12] = wc.reshape(3, 4, 128).transpose(2, 0, 1).reshape(128, 12)
    wf = np.asarray(w_ffn_conv, f)[0]
    cst[:, C_WF:C_WF + 132] = wf.reshape(3, 44, 128).transpose(2, 0, 1).reshape(128, 132)
    for c, w in enumerate(WINS):
        for t in range(16):
            cst[:, C_IC + 16 * c + t] = 1.0 / min(w, t + 1)

    shared = {
        "w_in": np.ascontiguousarray(win_b.reshape(128, -1)),
        "w_pool": np.ascontiguousarray(wpool_b.reshape(128, -1)),
        "w_co": np.ascontiguousarray(wco_b.reshape(128, -1)),
        "w_o": np.ascontiguousarray(wo_b.reshape(128, -1)),
        "w_up": np.ascontiguousarray(wup_b.reshape(128, -1)),
        "w_dn": np.ascontiguousarray(wdn_b.reshape(128, -1)),
        "cst": cst,
    }
    in_maps = []
    for b in range(NCORES):
        sl = slice(NS * b, NS * b + NS)
        X = np.concatenate([meta_tokens, x_prompt[b], x_sample[sl, 0]], axis=0)
        xt = X.reshape(NT, N, KC, 128).transpose(0, 3, 2, 1)
        m = dict(shared)
        m["xT"] = np.ascontiguousarray(xt.reshape(NT, 128, KC * N))
        m["st_pool"] = np.ascontiguousarray(
            state_pool[0, sl].reshape(NS, 15, 4, 128).transpose(3, 2, 1, 0).reshape(128, -1))
        m["st_conv"] = np.ascontiguousarray(
            state_conv[0, sl].reshape(NS, 2, 4, 128).transpose(3, 2, 1, 0).reshape(128, -1))
        m["st_ffn"] = np.ascontiguousarray(
            state_ffn[0, sl].reshape(NS, 2, 44, 128).transpose(3, 2, 1, 0).reshape(128, -1))
        in_maps.append(m)

    if "nc" not in _NC_CACHE:
        _NC_CACHE["nc"] = build_nc()
    nc = _NC_CACHE["nc"]
    res = run_bass_kernel_spmd(nc, in_maps, core_ids=list(range(NCORES)))
    outs = res.results

    NSAMP = NS * NCORES
    y_prompt = np.zeros((B, 2048, D), f)
    y_sample = np.zeros((NSAMP, 1, D), f)
    nsp_p = np.zeros((1, B, 15, 512), f)
    nsc_p = np.zeros((1, B, 2, 512), f)
    nsf_p = np.zeros((1, B, 2, 2 * DFF), f)
    nsp_s = np.zeros((1, NSAMP, 15, 512), f)
    nsc_s = np.zeros((1, NSAMP, 2, 512), f)
    nsf_s = np.zeros((1, NSAMP, 2, 2 * DFF), f)
    for b in range(NCORES):
        r = outs[b]
        sl = slice(NS * b, NS * b + NS)
        Y = r["yT"].reshape(NT, 128, KC, N).transpose(0, 3, 2, 1).reshape(NT * N, D)
        y_prompt[b] = Y[16:16 + 2048]
        y_sample[sl, 0] = Y[16 + 2048:]
        sm = r["o_small"]
        nsp_p[0, b] = sm[:, 0:60].reshape(128, 4, 15).transpose(2, 1, 0).reshape(15, 512)
        nsc_p[0, b] = sm[:, 60:68].reshape(128, 4, 2).transpose(2, 1, 0).reshape(2, 512)
        tail = r["o_ffn_tail"].reshape(128, 44, 18)
        nsf_p[0, b] = tail[:, :, 0:2].transpose(2, 1, 0).reshape(2, 2 * DFF)
        nsf_s[0, sl, 1] = tail[:, :, 2:18].transpose(2, 1, 0).reshape(NS, 2 * DFF)
        nsf_s[0, sl, 0] = r["o_ffn_st"].reshape(128, 44, 2, NS)[:, :, 1, :].transpose(2, 1, 0).reshape(NS, 2 * DFF)
        nsp_s[0, sl, 0:14] = r["o_pool_sa"].reshape(128, 4, 14, NS).transpose(3, 2, 1, 0).reshape(NS, 14, 512)
        nsp_s[0, sl, 14] = sm[:, 68:132].reshape(128, 4, NS).transpose(2, 1, 0).reshape(NS, 512)
        nsc_s[0, sl, 0] = r["o_conv_st"].reshape(128, 4, 2, NS)[:, :, 1, :].transpose(2, 1, 0).reshape(NS, 512)
        nsc_s[0, sl, 1] = sm[:, 132:196].reshape(128, 4, NS).transpose(2, 1, 0).reshape(NS, 512)
    return (y_prompt, y_sample, nsp_p, nsc_p, nsf_p, nsp_s, nsc_s, nsf_s)
```

```python
import numpy as np
import concourse.bass as bass
import concourse.mybir as mybir
from concourse.bass_utils import run_bass_kernel_spmd

F32 = mybir.dt.float32
BF16 = mybir.dt.bfloat16
AF = mybir.ActivationFunctionType
ALU = mybir.AluOpType
AX = mybir.AxisListType

NCORES = 8
NT = 8
N = 260
NH = N + 2
UH = N + 15
SC = 244
NS = 16
D = 1024
KC = 8
DFF = 2816
FC = 22
EPS = 1e-6
WINS = (2, 4, 8, 16)
SBW = 53120
NPF = 9
STRICT_SAME_ENGINE = True

WIN_ORDER = list(range(4, 8)) + list(range(12, 16)) + list(range(0, 4)) + list(range(16, 24)) \
    + list(range(24, 32)) + list(range(8, 12))

C_G1, C_G2, C_G3, C_G4, C_PS = 0, 8, 16, 24, 32
C_WC = 40
C_WF = 52
C_IC = 184
C_END = 248


class _Op:
    __slots__ = ("idx", "eng", "fn", "reads", "writes", "dma", "extra", "deps", "sig", "sem", "val", "nb", "after")


class Sched:
    ENGS = ("pe", "act", "dve", "pool", "sp")

    def __init__(self):
        self.ops = []
        self.dma_ops = []

    def add(self, eng, fn, reads=(), writes=(), dma=False, extra=(), nb=False, after=()):
        op = _Op()
        op.nb = nb
        op.after = tuple(after)
        op.idx = len(self.ops)
        op.eng, op.fn, op.reads, op.writes, op.dma = eng, fn, tuple(reads), tuple(writes), dma
        op.extra = tuple(extra)
        op.deps, op.sig, op.sem, op.val = set(), False, None, 0
        self.ops.append(op)
        if dma:
            self.dma_ops.append(op)
        return op

    def tails(self):
        last = {}
        for op in self.ops:
            if not op.dma and op.fn is not None:
                last[op.eng] = op
        return list(last.values()) + [d for d in self.dma_ops if not d.nb]

    def barrier(self):
        t = self.tails()
        for e in self.ENGS:
            self.add(e, None, extra=t)

    def finalize(self, nc, stack, n_sp=14, n_pool=6):
        ops = self.ops
        last_w, readers = {}, {}
        for op in ops:
            raw, oth = set(), set()
            for t in op.reads:
                if t in last_w:
                    raw.add(last_w[t])
            for t in op.writes:
                if t in last_w:
                    oth.add(last_w[t])
                oth.update(readers.get(t, ()))
            for t in op.after:
                if t in last_w:
                    oth.add(last_w[t])
                oth.update(readers.get(t, ()))
            for x in op.extra:
                raw.add(x.idx)
            for t in op.reads:
                readers.setdefault(t, []).append(op.idx)
            for t in op.writes:
                last_w[t] = op.idx
                readers[t] = []
            deps = set(raw)
            for d in oth:
                dop = ops[d]
                if dop.eng == op.eng and not dop.dma and not op.dma and (not STRICT_SAME_ENGINE or op.eng == "pe"):
                    continue
                deps.add(d)
            deps.discard(op.idx)
            op.deps = deps
        sems = {e: stack.enter_context(nc.semaphore("sem_" + e)) for e in ("pe", "act", "dve", "pool")}
        dsem = {"sp": [stack.enter_context(nc.semaphore("dsp%d" % i)) for i in range(n_sp)],
                "pool": [stack.enter_context(nc.semaphore("dpl%d" % i)) for i in range(n_pool)]}
        cnt = {"sp": 0, "pool": 0}
        prev_use = {}
        uses = {}
        for op in ops:
            if op.dma:
                lst = dsem[op.eng]
                s = lst[cnt[op.eng] % len(lst)]
                cnt[op.eng] += 1
                key = id(s)
                uses[key] = uses.get(key, 0) + 1
                op.sem, op.val, op.sig = s, 16 * uses[key], True
                if key in prev_use:
                    op.deps.add(prev_use[key])
                prev_use[key] = op.idx
        for op in ops:
            for d in op.deps:
                ops[d].sig = True
        ccnt = {e: 0 for e in sems}
        for op in ops:
            if op.dma:
                continue
            if op.sig:
                assert op.fn is not None, "nop cannot signal"
                ccnt[op.eng] += 1
                op.sem, op.val = sems[op.eng], ccnt[op.eng]
        self.sems = sems

    def emit(self, eng, e):
        known = {}
        ops = self.ops
        for op in ops:
            if op.eng != eng:
                continue
            need = {}
            for d in op.deps:
                dop = ops[d]
                k = id(dop.sem)
                if k not in need or need[k][1] < dop.val:
                    need[k] = (dop.sem, dop.val)
            for k, (s, v) in need.items():
                if known.get(k, 0) >= v:
                    continue
                e.wait_ge(s, v)
                known[k] = v
            if op.fn is None:
                continue
            ins = op.fn(e)
            if op.sig:
                ins.then_inc(op.sem, 16 if op.dma else 1)


def build_nc():
    from contextlib import ExitStack
    nc = bass.Bass("TRN2", target_bir_lowering=False)

    def din(name, shape):
        return nc.dram_tensor(name, list(shape), F32, kind="ExternalInput").ap()

    def dout(name, shape):
        return nc.dram_tensor(name, list(shape), F32, kind="ExternalOutput").ap()

    xT = din("xT", [NT, 128, KC * N])
    w_in = din("w_in", [128, 32 * KC * 128])
    w_pool = din("w_pool", [128, 8 * 128])
    w_co = din("w_co", [128, 8 * 4 * 128])
    w_o = din("w_o", [128, 8 * KC * 128])
    w_up = din("w_up", [128, 44 * KC * 128])
    w_dn = din("w_dn", [128, 8 * FC * 128])
    cst = din("cst", [128, 256])
    st_pool_d = din("st_pool", [128, 4 * 15 * NS])
    st_conv_d = din("st_conv", [128, 4 * 2 * NS])
    st_ffn_d = din("st_ffn", [128, 44 * 2 * NS])

    yT = dout("yT", [NT, 128, KC * N])
    o_small = dout("o_small", [128, 196])
    o_ffn_tail = dout("o_ffn_tail", [128, 44 * 18])
    o_pool_sa = dout("o_pool_sa", [128, 4, 14 * NS])
    o_conv_st = dout("o_conv_st", [128, 4 * 2 * NS])
    o_ffn_st = dout("o_ffn_st", [128, 44 * 2 * NS])
    x1T = nc.dram_tensor("x1T", [NT, 128, KC * N], F32).ap()

    stack = ExitStack()
    big = stack.enter_context(nc.sbuf_tensor("big", [128, SBW], F32))
    ps = stack.enter_context(nc.psum_tensor("ps", [128, 4096], F32))

    def R(off, n):
        return big[:, off:off + n]

    def RB(off, n_words):
        return big[:, off:off + n_words].bitcast(BF16)

    def bank(b, n=N):
        return ps[:, b * 512:b * 512 + n]

    CST = R(0, 256)
    o = 256
    ST_POOL = R(o, 960).rearrange("p (c r s) -> p c r s", c=4, r=15); o += 960
    ST_CONV = R(o, 128).rearrange("p (c r s) -> p c r s", c=4, r=2); o += 128
    ST_FFN = R(o, 1408).rearrange("p (c r s) -> p c r s", c=44, r=2); o += 1408
    PRE = R(o, 704).rearrange("p (c s) -> p c s", c=44); o += 704
    RED = R(o, 64).rearrange("p (c s) -> p c s", c=4); o += 64
    ONES = RB(o, 64); o += 64
    SD = [R(o, N), R(o + N, N)]; o += 2 * N
    COMMON_END = o

    AREG = []
    claimed = set()

    def areg(lo, n, tok):
        AREG.append((lo, lo + n, tok))

    def alias(lo, n):
        if (lo, n) in claimed:
            return []
        claimed.add((lo, n))
        return [t for (a, b, t) in AREG if a < lo + n and lo < b]

    areg(256, 960, "st_pool")
    areg(1216, 128, "st_conv")
    o = COMMON_END
    W_IN = RB(o, 16384).rearrange("p (j k c) -> p j k c", j=32, k=KC); WIN_OFF = o; o += 16384
    W_POOL = RB(o, 512).rearrange("p (j c) -> p j c", j=8); WPOOL_OFF = o; areg(o, 512, "w_pool"); o += 512
    W_CO = RB(o, 2048).rearrange("p (j k c) -> p j k c", j=8, k=4); WCO_OFF = o; areg(o, 2048, "w_co"); o += 2048
    W_O = RB(o, 4096).rearrange("p (j k c) -> p j k c", j=8, k=KC); WO_OFF = o
    for j in range(8):
        areg(o + 512 * j, 512, ("w_o", j))
    o += 4096
    XB = []
    for i in range(2):
        XB.append(R(o, KC * N).rearrange("p (k n) -> p k n", k=KC))
        for k in range(KC):
            areg(o + k * N, N, ("x", i, k))
        o += KC * N
    XSQ = RB(o, KC * N // 2).rearrange("p (k n) -> p k n", k=KC); XSQ_OFF = o
    for k in range(KC):
        areg(o + k * (N // 2), N // 2, ("xsq", k))
    o += KC * N // 2
    HB = []
    for i in range(2):
        HB.append(RB(o, KC * N // 2).rearrange("p (k n) -> p k n", k=KC))
        for k in range(KC):
            areg(o + k * (N // 2), N // 2, ("h", i, k))
        o += KC * N // 2
    UB = R(o, 4 * UH).rearrange("p (c n) -> p c n", c=4)
    for c in range(4):
        areg(o + c * UH, UH, ("u", c))
    o += 4 * UH
    PB = R(o, 4 * NH).rearrange("p (c n) -> p c n", c=4)
    for c in range(4):
        areg(o + c * NH, NH, ("p", c))
    o += 4 * NH
    CV = R(o, 4 * N).rearrange("p (c n) -> p c n", c=4)
    for c in range(4):
        areg(o + c * N, N, ("cv", c))
    o += 4 * N
    SGA = R(o, 8 * N).rearrange("p (c n) -> p c n", c=8)
    for c in range(8):
        areg(o + c * N, N, ("sga", c))
    o += 8 * N
    SGB = R(o, 8 * N).rearrange("p (c n) -> p c n", c=8)
    for c in range(8):
        areg(o + c * N, N, ("sgb", c))
    o += 8 * N
    PT = []
    for i in range(2):
        PT.append(R(o, 4 * UH).rearrange("p (c n) -> p c n", c=4)); areg(o, 4 * UH, ("pt", i)); o += 4 * UH
    POOLED = RB(o, 4 * N // 2).rearrange("p (c n) -> p c n", c=4)
    for c in range(4):
        areg(o + c * (N // 2), N // 2, ("pooled", c))
    o += 4 * N // 2
    BGCV = RB(o, 4 * N // 2).rearrange("p (c n) -> p c n", c=4)
    for c in range(4):
        areg(o + c * (N // 2), N // 2, ("bgcv", c))
    o += 4 * N // 2
    MERGED = RB(o, 8 * N // 2).rearrange("p (c n) -> p c n", c=8)
    for c in range(8):
        areg(o + c * (N // 2), N // 2, ("merged", c))
    o += 8 * N // 2
    RBUF = R(o, 8 * N).rearrange("p (c n) -> p c n", c=8)
    for c in range(8):
        areg(o + c * N, N, ("r", c))
    o += 8 * N
    TMPF = R(o, 16); areg(o, 16, "tmpf"); o += 16
    STG = R(o, 196); areg(o, 196, "stg"); o += 196
    A_END = o
    PF_OFF = SBW - NPF * 512
    assert PF_OFF >= A_END, (PF_OFF, A_END)

    NWIN = 32
    o = WIN_OFF + 16384
    WUP_C_OFF = o
    o += (44 - NPF - NWIN) * 512
    WDN_OFF = o
    W_DN = RB(o, 11264).rearrange("p (j k c) -> p j k c", j=8, k=FC); o += 11264
    X1B = R(o, KC * N).rearrange("p (k n) -> p k n", k=KC); X1B_OFF = o; o += KC * N
    XSQB = RB(o, KC * N // 2).rearrange("p (k n) -> p k n", k=KC); XSQB_OFF = o; o += KC * N // 2
    YSQ = RB(o, KC * N // 2).rearrange("p (k n) -> p k n", k=KC); YSQ_OFF = o; o += KC * N // 2
    H2, H2_OFF = [], []
    for i in range(2):
        H2.append(RB(o, KC * NH // 2).rearrange("p (k n) -> p k n", k=KC)); H2_OFF.append(o); o += KC * NH // 2
    GB, GB_OFF = [], []
    for i in range(2):
        GB.append(RB(o, FC * N // 2).rearrange("p (c n) -> p c n", c=FC)); GB_OFF.append(o); o += FC * N // 2
    NSET = 4
    TG, TV, SG, TG_OFF, TV_OFF = [], [], [], [], []
    for i in range(2):
        TG.append(R(o, N)); TG_OFF.append(o); o += N
        TV.append(R(o, N)); TV_OFF.append(o); o += N
    for i in range(2):
        TG.append(R(256 + 2 * i * N, N)); TV.append(R(256 + (2 * i + 1) * N, N))
        TG_OFF.append(256 + 2 * i * N); TV_OFF.append(256 + (2 * i + 1) * N)
    assert 256 + 4 * N <= 1344
    for i in range(NSET):
        SG.append(TG[i])
    YB = R(o, 8 * N).rearrange("p (c n) -> p c n", c=8); YB_OFF = o; o += 8 * N
    TAIL = R(X1B_OFF, 44 * 18).rearrange("p (c n) -> p c n", c=44); TAIL_FLAT = R(X1B_OFF, 44 * 18)
    assert o <= PF_OFF, (o, PF_OFF)

    def wup_off(b):
        if b < NPF:
            return PF_OFF + 512 * b
        if b < NPF + NWIN:
            return WIN_OFF + 512 * (b - NPF)
        return WUP_C_OFF + 512 * (b - NPF - NWIN)

    def wup_blk(b):
        return RB(wup_off(b), 512).rearrange("p (k c) -> p k c", k=KC)

    def cc(col, n=1):
        return CST[:, col:col + n]

    def bc8(ap2d):
        return ap2d.unsqueeze(1).to_broadcast([128, 8, N])

    S = Sched()
    bank_rr = [0]

    def next_bank():
        b = bank_rr[0]
        bank_rr[0] = (b + 1) % 7
        return b

    SB = 7

    def load_x(ti):
        S.add("sp", lambda e: e.dma_start(out=XB[ti % 2].rearrange("p k n -> p (k n)"), in_=xT[ti]),
              writes=[("x", ti % 2, k) for k in range(KC)], dma=True)

    load_x(0)
    S.add("sp", lambda e: e.dma_start(out=CST, in_=cst), writes=["cst"], dma=True)
    load_x(1)
    def load_states():
        S.add("sp", lambda e: e.dma_start(out=R(256, 960), in_=st_pool_d), writes=["st_pool"], dma=True)
        S.add("sp", lambda e: e.dma_start(out=R(1216, 128), in_=st_conv_d), writes=["st_conv"], dma=True)
        S.add("sp", lambda e: e.dma_start(out=R(1344, 1408), in_=st_ffn_d), writes=["st_ffn"], dma=True)
        S.add("sp", lambda e: e.dma_start(out=o_pool_sa, in_=ST_POOL[:, :, 1:15, :].rearrange("p c r s -> p c (r s)")),
              reads=["st_pool"], writes=["o3"], dma=True)
        S.add("sp", lambda e: e.dma_start(out=o_conv_st, in_=R(1216, 128)), reads=["st_conv"], writes=["o5"], dma=True)
        S.add("sp", lambda e: e.dma_start(out=o_ffn_st, in_=R(1344, 1408)), reads=["st_ffn"], writes=["o7"], dma=True)


    def wload(dst_off, src, src_off_elems, n_elems, toks, nb=False, al=False):
        dst = RB(dst_off, n_elems // 2)
        extra_w = alias(dst_off, n_elems // 2) if al else []
        S.add("pool", lambda e: e.dma_start(out=dst, in_=src[:, src_off_elems:src_off_elems + n_elems],
                                            max_dma_last_dim=8192),
              writes=list(toks), after=extra_w, dma=True, nb=nb)

    BL = KC * 128
    S.add("pool", lambda e: e.memset(ONES, 1.0), writes=["ones"])
    S.add("pool", lambda e: e.memset(UB[:, :, 0:15], 0.0), writes=[("u", c) for c in range(4)])
    S.add("pool", lambda e: e.memset(PB[:, :, 0:2], 0.0), writes=[("p", c) for c in range(4)])
    for q in range(8):
        wload(WIN_OFF + q * 4 * BL // 2, w_in, q * 4 * BL, 4 * BL, [("w_in", j) for j in range(4 * q, 4 * q + 4)])
    wload(WPOOL_OFF, w_pool, 0, 1024, ["w_pool"])
    wload(WCO_OFF, w_co, 0, 4096, ["w_co"])
    for q in range(2):
        wload(WO_OFF + q * 2048, w_o, q * 4096, 4096, [("w_o", j) for j in range(4 * q, 4 * q + 4)])
    b0 = 0
    while b0 < NPF:
        nb = min(4, NPF - b0)
        wload(wup_off(b0), w_up, b0 * BL, nb * BL, [("w_up", b0 + i) for i in range(nb)])
        b0 += nb

    EPSB = CST[:, 255:256]
    S.add("dve", lambda e: e.memset(EPSB, EPS), reads=["cst"], writes=["eps"])
    S.add("act", lambda e: e.activation(out=TMPF[:, 0:2], in_=ONES[:, 0:2], func=AF.Sqrt),
          reads=["ones"], writes=["tmpf"])

    def norm_stats(sq, sq_tok, sd_i):
        def mm(e):
            for k in range(KC):
                ins = e.matmul(bank(SB), lhsT=ONES, rhs=sq[:, k, :], start=(k == 0), stop=(k == KC - 1))
            return ins
        S.add("pe", mm, reads=["ones"] + [(sq_tok, k) for k in range(KC)], writes=[("ps", SB)])
        S.add("act", lambda e: e.activation(out=SD[sd_i], in_=bank(SB), func=AF.Sqrt, scale=1.0 / D, bias=EPSB),
              reads=[("ps", SB), "eps"], writes=[("sd", sd_i)])
        S.add("dve", lambda e: e.reciprocal(out=SD[sd_i], in_=SD[sd_i]),
              reads=[("sd", sd_i)], writes=[("sd", sd_i)])

    def prenorm_A(ti):
        xb = XB[ti % 2]
        hb = HB[ti % 2]
        S.add("act", lambda e: e.activation(out=XSQ, in_=xb, func=AF.Square),
              reads=[("x", ti % 2, k) for k in range(KC)], writes=[("xsq", k) for k in range(KC)])
        norm_stats(XSQ, "xsq", 0)
        for k in range(KC):
            S.add("dve", lambda e, k=k: e.scalar_tensor_tensor(
                out=hb[:, k, :], in0=xb[:, k, :], scalar=cc(C_G1 + k), in1=SD[0], op0=ALU.mult, op1=ALU.mult),
                reads=[("x", ti % 2, k), ("sd", 0), "cst"], writes=[("h", ti % 2, k)])

    def win_group(ti, jj):
        hb = HB[ti % 2]
        b = next_bank()

        def mm(e):
            for k in range(KC):
                ins = e.matmul(bank(b), lhsT=W_IN[:, jj, k, :], rhs=hb[:, k, :], start=(k == 0), stop=(k == KC - 1))
            return ins
        S.add("pe", mm, reads=[("w_in", jj)] + [("h", ti % 2, k) for k in range(KC)], writes=[("ps", b)])
        return b

    def A_part1(ti):
        for c in range(4):
            b = win_group(ti, c)
            S.add("act", lambda e, b=b, c=c: e.activation(out=PB[:, c, 2:NH], in_=bank(b), func=AF.Copy),
                  reads=[("ps", b)], writes=[("p", c)])
        for c in range(4):
            b = win_group(ti, 4 + c)
            S.add("dve", lambda e, b=b, c=c: e.tensor_tensor(out=PB[:, c, 2:NH], in0=bank(b), in1=PB[:, c, 2:NH],
                                                             op=ALU.mult),
                  reads=[("ps", b), ("p", c)], writes=[("p", c)])
        for c in range(4):
            b = win_group(ti, 8 + c)
            S.add("act", lambda e, b=b, c=c: e.activation(out=UB[:, c, 15:UH], in_=bank(b), func=AF.Copy),
                  reads=[("ps", b)], writes=[("u", c)])

    def A_epilogue(ti):
        last = (ti == NT - 1)
        xb = XB[ti % 2]
        norm_stats(XSQ, "xsq", 1)
        S.add("dve", lambda e: e.tensor_tensor(out=RBUF, in0=RBUF, in1=bc8(SD[1]), op=ALU.mult),
              reads=[("r", m) for m in range(8)] + [("sd", 1)], writes=[("r", m) for m in range(8)])
        S.add("dve", lambda e: e.tensor_tensor(out=xb, in0=xb, in1=RBUF, op=ALU.add),
              reads=[("r", m) for m in range(8)] + [("x", ti % 2, m) for m in range(8)],
              writes=[("x", ti % 2, m) for m in range(8)])
        S.add("sp", lambda e: e.dma_start(out=x1T[ti], in_=xb.rearrange("p k n -> p (k n)")),
              reads=[("x", ti % 2, k) for k in range(KC)], writes=[("x1T", ti)], dma=True)
        if ti + 2 < NT:
            load_x(ti + 2)

    def A_part2(ti):
        last = (ti == NT - 1)
        if last:
            for c, w in enumerate(WINS):
                if w == 2:
                    S.add("dve", lambda e, c=c: e.tensor_copy(out=RED[:, c, :], in_=ST_POOL[:, c, 14, :]),
                          reads=["st_pool"], writes=[("red", c)])
                else:
                    S.add("dve", lambda e, c=c, w=w: e.tensor_reduce(
                        out=RED[:, c, :], in_=ST_POOL[:, c, 16 - w:15, :].rearrange("p r s -> p s r"),
                        axis=AX.X, op=ALU.add), reads=["st_pool"], writes=[("red", c)])
        for c in range(4):
            S.add("act", lambda e, c=c: e.activation(out=CV[:, c, :], in_=PB[:, c, 0:N], func=AF.Copy,
                                                     scale=cc(C_WC + c)),
                  reads=[("p", c), "cst"], writes=[("cv", c)])
            S.add("dve", lambda e, c=c: e.scalar_tensor_tensor(out=CV[:, c, :], in0=PB[:, c, 1:N + 1],
                                                               scalar=cc(C_WC + 4 + c), in1=CV[:, c, :],
                                                               op0=ALU.mult, op1=ALU.add),
                  reads=[("p", c), ("cv", c), "cst"], writes=[("cv", c)])
            S.add("dve", lambda e, c=c: e.scalar_tensor_tensor(out=CV[:, c, :], in0=PB[:, c, 2:N + 2],
                                                               scalar=cc(C_WC + 8 + c), in1=CV[:, c, :],
                                                               op0=ALU.mult, op1=ALU.add),
                  reads=[("p", c), ("cv", c), "cst"], writes=[("cv", c)])
            if last:
                S.add("dve", lambda e, c=c: e.tensor_scalar(out=CV[:, c, SC:N], in0=ST_CONV[:, c, 0, :],
                                                            scalar1=cc(C_WC + c), scalar2=None, op0=ALU.mult),
                      reads=["st_conv", "cst", ("cv", c)], writes=[("cv", c)])
                S.add("dve", lambda e, c=c: e.scalar_tensor_tensor(out=CV[:, c, SC:N], in0=ST_CONV[:, c, 1, :],
                                                                   scalar=cc(C_WC + 4 + c), in1=CV[:, c, SC:N],
                                                                   op0=ALU.mult, op1=ALU.add),
                      reads=["st_conv", "cst", ("cv", c)], writes=[("cv", c)])
                S.add("dve", lambda e, c=c: e.scalar_tensor_tensor(out=CV[:, c, SC:N], in0=PB[:, c, 2 + SC:NH],
                                                                   scalar=cc(C_WC + 8 + c), in1=CV[:, c, SC:N],
                                                                   op0=ALU.mult, op1=ALU.add),
                      reads=[("p", c), "cst", ("cv", c)], writes=[("cv", c)])
        for c in range(8):
            b = win_group(ti, 12 + c)
            S.add("act", lambda e, b=b, c=c: e.activation(out=SGA[:, c, :], in_=bank(b), func=AF.Sigmoid),
                  reads=[("ps", b)], writes=[("sga", c)])
        utoks = [("u", c) for c in range(4)]
        S.add("dve", lambda e: e.tensor_tensor(out=PT[0][:, 0:4, 1:UH], in0=UB[:, 0:4, 1:UH], in1=UB[:, 0:4, 0:UH - 1],
                                               op=ALU.add), reads=utoks, writes=[("pt", 0)])
        S.add("dve", lambda e: e.tensor_tensor(out=PT[1][:, 1:4, 3:UH], in0=PT[0][:, 1:4, 3:UH],
                                               in1=PT[0][:, 1:4, 1:UH - 2], op=ALU.add),
              reads=[("pt", 0)], writes=[("pt", 1)])
        S.add("dve", lambda e: e.tensor_tensor(out=PT[0][:, 2:4, 7:UH], in0=PT[1][:, 2:4, 7:UH],
                                               in1=PT[1][:, 2:4, 3:UH - 4], op=ALU.add),
              reads=[("pt", 1), ("pt", 0)], writes=[("pt", 0)])
        S.add("dve", lambda e: e.tensor_tensor(out=PT[1][:, 3, 15:UH], in0=PT[0][:, 3, 15:UH],
                                               in1=PT[0][:, 3, 7:UH - 8], op=ALU.add),
              reads=[("pt", 0), ("pt", 1)], writes=[("pt", 1)])
        for c, w in enumerate(WINS):
            cur = PT[c % 2][:, c, :]
            ptoks = [("pt", 0), ("pt", 1)]
            S.add("dve", lambda e, c=c, w=w, cur=cur: e.scalar_tensor_tensor(
                out=POOLED[:, c, :], in0=cur[:, 15:UH], scalar=1.0 / w, in1=UB[:, c, 15:UH],
                op0=ALU.mult, op1=ALU.subtract),
                reads=ptoks + [("u", c)], writes=[("pooled", c)])
            if ti == 0:
                S.add("dve", lambda e, c=c, cur=cur: e.tensor_tensor(out=TMPF, in0=cur[:, 15:31],
                                                                     in1=CST[:, C_IC + 16 * c:C_IC + 16 * c + 16],
                                                                     op=ALU.mult),
                      reads=ptoks + ["cst"], writes=["tmpf"])
                S.add("dve", lambda e, c=c: e.tensor_tensor(out=POOLED[:, c, 0:16], in0=TMPF, in1=UB[:, c, 15:31],
                                                            op=ALU.subtract),
                      reads=["tmpf", ("u", c), ("pooled", c)], writes=[("pooled", c)])
            if last:
                S.add("dve", lambda e, c=c: e.tensor_tensor(out=TMPF, in0=RED[:, c, :], in1=UB[:, c, 15 + SC:UH],
                                                            op=ALU.add),
                      reads=[("red", c), ("u", c)], writes=["tmpf"])
                S.add("dve", lambda e, c=c, w=w: e.scalar_tensor_tensor(
                    out=POOLED[:, c, SC:N], in0=TMPF, scalar=1.0 / w, in1=UB[:, c, 15 + SC:UH],
                    op0=ALU.mult, op1=ALU.subtract),
                    reads=["tmpf", ("u", c), ("pooled", c)], writes=[("pooled", c)])
        for c in range(8):
            b = win_group(ti, 20 + c)
            S.add("act", lambda e, b=b, c=c: e.activation(out=SGB[:, c, :], in_=bank(b), func=AF.Sigmoid),
                  reads=[("ps", b)], writes=[("sgb", c)])
        for c in range(4):
            b = win_group(ti, 28 + c)
            S.add("dve", lambda e, b=b, c=c: e.tensor_tensor(out=BGCV[:, c, :], in0=bank(b), in1=CV[:, c, :],
                                                             op=ALU.mult),
                  reads=[("ps", b), ("cv", c)], writes=[("bgcv", c)])
        if last:
            early_w_up(NPF, NPF + 12)
        if not last:
            S.add("pool", lambda e: e.tensor_copy(out=UB[:, :, 0:15], in_=UB[:, :, N:N + 15]),
                  reads=[("u", c) for c in range(4)], writes=[("u", c) for c in range(4)])
            S.add("pool", lambda e: e.tensor_copy(out=PB[:, :, 0:2], in_=PB[:, :, N:N + 2]),
                  reads=[("p", c) for c in range(4)], writes=[("p", c) for c in range(4)])
        else:
            S.add("pool", lambda e: e.tensor_copy(out=STG[:, 0:60].rearrange("p (c n) -> p c n", c=4), in_=UB[:, :, 244:259]),
                  reads=[("u", c) for c in range(4)], writes=["stg"])
            S.add("pool", lambda e: e.tensor_copy(out=STG[:, 60:68].rearrange("p (c n) -> p c n", c=4), in_=PB[:, :, 244:246]),
                  reads=[("p", c) for c in range(4)], writes=["stg"])
            S.add("pool", lambda e: e.tensor_copy(out=STG[:, 68:132].rearrange("p (c n) -> p c n", c=4), in_=UB[:, :, 15 + SC:UH]),
                  reads=[("u", c) for c in range(4)], writes=["stg"])
            S.add("pool", lambda e: e.tensor_copy(out=STG[:, 132:196].rearrange("p (c n) -> p c n", c=4), in_=PB[:, :, 2 + SC:NH]),
                  reads=[("p", c) for c in range(4)], writes=["stg"])
            S.add("sp", lambda e: e.dma_start(out=o_small, in_=STG), reads=["stg"], writes=["o1"], dma=True)
        for m in range(8):
            b = next_bank()
            S.add("pe", lambda e, b=b, m=m: e.matmul(bank(b), lhsT=W_POOL[:, m, :], rhs=POOLED[:, m // 2, :],
                                                     start=True, stop=True),
                  reads=["w_pool", ("pooled", m // 2)], writes=[("ps", b)])
            S.add("dve", lambda e, b=b, m=m: e.scalar_tensor_tensor(out=SGA[:, m, :], in0=bank(b),
                                                                    scalar=cc(C_PS + m), in1=SGA[:, m, :],
                                                                    op0=ALU.mult, op1=ALU.mult),
                  reads=[("ps", b), ("sga", m), "cst"], writes=[("sga", m)])
        for m in range(8):
            b = next_bank()

            def mm(e, b=b, m=m):
                for k in range(4):
                    ins = e.matmul(bank(b), lhsT=W_CO[:, m, k, :], rhs=BGCV[:, k, :], start=(k == 0), stop=(k == 3))
                return ins
            S.add("pe", mm, reads=["w_co"] + [("bgcv", k) for k in range(4)], writes=[("ps", b)])
            S.add("dve", lambda e, b=b, m=m: e.tensor_tensor(out=SGB[:, m, :], in0=bank(b), in1=SGB[:, m, :],
                                                             op=ALU.mult),
                  reads=[("ps", b), ("sgb", m)], writes=[("sgb", m)])
            if m % 4 == 3:
                m0 = m - 3
                S.add("dve", lambda e, m0=m0: e.tensor_tensor(out=MERGED[:, m0:m0 + 4, :], in0=SGA[:, m0:m0 + 4, :],
                                                              in1=SGB[:, m0:m0 + 4, :], op=ALU.add),
                      reads=[("sga", q) for q in range(m0, m0 + 4)] + [("sgb", q) for q in range(m0, m0 + 4)],
                      writes=[("merged", q) for q in range(m0, m0 + 4)])
        if ti + 1 < NT:
            prenorm_A(ti + 1)
        else:
            start_ffn_phase()
        NSPLIT = 6
        HK = KC // 2
        wbanks = [next_bank() for _ in range(NSPLIT)]

        def r_evac(bk, m):
            S.add("act", lambda e: e.activation(out=RBUF[:, m, :], in_=bank(bk), func=AF.Copy, scale=cc(C_G2 + m)),
                  reads=[("ps", bk), "cst"], writes=[("r", m)])
            S.add("act", lambda e: e.activation(out=XSQ[:, m, :], in_=bank(bk), func=AF.Square),
                  reads=[("ps", bk)], writes=[("xsq", m)])

        for m in range(NSPLIT):
            def mm1(e, bk=wbanks[m], m=m):
                for k in range(HK):
                    ins = e.matmul(bank(bk), lhsT=W_O[:, m, k, :], rhs=MERGED[:, k, :], start=(k == 0), stop=False)
                return ins
            S.add("pe", mm1, reads=[("w_o", m)] + [("merged", k) for k in range(HK)], writes=[("ps", wbanks[m])])
        for m in range(NSPLIT):
            def mm2(e, bk=wbanks[m], m=m):
                for k in range(HK, KC):
                    ins = e.matmul(bank(bk), lhsT=W_O[:, m, k, :], rhs=MERGED[:, k, :], start=False, stop=(k == KC - 1))
                return ins
            S.add("pe", mm2, reads=[("w_o", m)] + [("merged", k) for k in range(HK, KC)], writes=[("ps", wbanks[m])])
            r_evac(wbanks[m], m)
        for m in range(NSPLIT, 8):
            bk = next_bank()

            def mm(e, bk=bk, m=m):
                for k in range(KC):
                    ins = e.matmul(bank(bk), lhsT=W_O[:, m, k, :], rhs=MERGED[:, k, :], start=(k == 0), stop=(k == KC - 1))
                return ins
            S.add("pe", mm, reads=[("w_o", m)] + [("merged", k) for k in range(KC)], writes=[("ps", bk)])
            r_evac(bk, m)

    def load_x1(ti):
        S.add("sp", lambda e: e.dma_start(out=X1B.rearrange("p k n -> p (k n)"), in_=x1T[ti]),
              reads=[("x1T", ti)], writes=[("x1", k) for k in range(KC)], after=alias(X1B_OFF, KC * N), dma=True)

    def y_init(ti):
        S.add("sp", lambda e: e.dma_start(out=yT[ti], in_=x1T[ti]),
              reads=[("x1T", ti)], writes=[("yT", ti)], dma=True)

    def early_w_up(b_lo, b_hi, extra=()):
        b0 = b_lo
        while b0 < b_hi:
            dst = RB(wup_off(b0), 4 * BL // 2)
            S.add("pool", lambda e, dst=dst, b0=b0: e.dma_start(out=dst, in_=w_up[:, b0 * BL:(b0 + 4) * BL],
                                                                max_dma_last_dim=8192),
                  writes=[("w_up", b0 + i) for i in range(4)] + [("w_in", b0 - NPF + i) for i in range(4)],
                  dma=True, extra=extra)
            b0 += 4

    def start_ffn_phase():
        load_x1(0)
        x1_load = S.ops[-1]
        S.add("pool", lambda e: e.memset(H2[0][:, :, 0:2], 0.0),
              writes=[("h2", 0, k) for k in range(KC)], after=alias(H2_OFF[0], KC * NH // 2))
        early_w_up(NPF + 12, NPF + NWIN, extra=[x1_load])
        for st in range(16):
            prenormB_piece(0, st)

    def prenorm_B(ti):
        hb = H2[ti % 2]
        S.add("act", lambda e: e.activation(out=XSQB, in_=X1B, func=AF.Square),
              reads=[("x1", k) for k in range(KC)], writes=[("xsqb", k) for k in range(KC)])
        norm_stats(XSQB, "xsqb", 0)
        if ti > 0:
            S.add("pool", lambda e: e.tensor_copy(out=hb[:, :, 0:2], in_=H2[(ti - 1) % 2][:, :, N:NH]),
                  reads=[("h2", (ti - 1) % 2, k) for k in range(KC)], writes=[("h2", ti % 2, k) for k in range(KC)])
        for k in range(KC):
            S.add("dve", lambda e, k=k: e.scalar_tensor_tensor(
                out=hb[:, k, 2:NH], in0=X1B[:, k, :], scalar=cc(C_G3 + k), in1=SD[0], op0=ALU.mult, op1=ALU.mult),
                reads=[("x1", k), ("sd", 0), "cst"], writes=[("h2", ti % 2, k)])
        if ti + 1 < NT:
            load_x1(ti + 1)

    HN = N // 2

    def prenormB_piece(ti, step):
        hb = H2[ti % 2]
        if step in (0, 1):
            c0 = 4 * step
            S.add("act", lambda e: e.activation(out=XSQB[:, c0:c0 + 4, :], in_=X1B[:, c0:c0 + 4, :], func=AF.Square),
                  reads=[("x1", k) for k in range(c0, c0 + 4)],
                  writes=[("xsqb", k) for k in range(c0, c0 + 4)], after=alias(XSQB_OFF + c0 * (N // 2), 4 * (N // 2)))
        elif step == 2:
            def mm(e):
                for k in range(KC):
                    ins = e.matmul(bank(SB), lhsT=ONES, rhs=XSQB[:, k, :], start=(k == 0), stop=(k == KC - 1))
                return ins
            S.add("pe", mm, reads=["ones"] + [("xsqb", k) for k in range(KC)], writes=[("ps", SB)])
            S.add("act", lambda e: e.activation(out=SD[0], in_=bank(SB), func=AF.Sqrt, scale=1.0 / D, bias=EPSB),
                  reads=[("ps", SB), "eps"], writes=[("sd", 0)])
            if ti > 0:
                S.add("pool", lambda e: e.tensor_copy(out=hb[:, :, 0:2], in_=H2[(ti - 1) % 2][:, :, N:NH]),
                      reads=[("h2", (ti - 1) % 2, k) for k in range(KC)],
                      writes=[("h2", ti % 2, k) for k in range(KC)], after=alias(H2_OFF[ti % 2], KC * NH // 2))
        elif step in (3, 4):
            h = step - 3
            S.add("dve", lambda e: e.reciprocal(out=SD[0][:, h * HN:(h + 1) * HN], in_=SD[0][:, h * HN:(h + 1) * HN]),
                  reads=[("sd", 0)], writes=[("sd", 0)])
        elif 5 <= step < 13:
            k = step - 5
            S.add("act", lambda e: e.activation(out=X1B[:, k, :], in_=X1B[:, k, :], func=AF.Copy, scale=cc(C_G3 + k)),
                  reads=[("x1", k), "cst"], writes=[("x1", k)])
        elif step in (13, 14):
            c0 = 4 * (step - 13)
            S.add("dve", lambda e: e.tensor_tensor(out=hb[:, c0:c0 + 4, 2:NH], in0=X1B[:, c0:c0 + 4, :],
                                                   in1=SD[0].unsqueeze(1).to_broadcast([128, 4, N]), op=ALU.mult),
                  reads=[("x1", k) for k in range(c0, c0 + 4)] + [("sd", 0)],
                  writes=[("h2", ti % 2, k) for k in range(c0, c0 + 4)])
        elif step == 15:
            if ti + 1 < NT:
                load_x1(ti + 1)

    def epilogueB_piece(ti, step):
        if step == 0:
            def mm(e):
                for k in range(KC):
                    ins = e.matmul(bank(SB), lhsT=ONES, rhs=YSQ[:, k, :], start=(k == 0), stop=(k == KC - 1))
                return ins
            S.add("pe", mm, reads=["ones"] + [("ysq", k) for k in range(KC)], writes=[("ps", SB)])
            S.add("act", lambda e: e.activation(out=SD[1], in_=bank(SB), func=AF.Sqrt, scale=1.0 / D, bias=EPSB),
                  reads=[("ps", SB), "eps"], writes=[("sd", 1)])
        elif step in (1, 2):
            h = step - 1
            S.add("dve", lambda e: e.reciprocal(out=SD[1][:, h * HN:(h + 1) * HN], in_=SD[1][:, h * HN:(h + 1) * HN]),
                  reads=[("sd", 1)], writes=[("sd", 1)])
        elif step in (3, 4):
            c0 = 4 * (step - 3)
            S.add("dve", lambda e: e.tensor_tensor(out=YB[:, c0:c0 + 4, :], in0=YB[:, c0:c0 + 4, :],
                                                   in1=SD[1].unsqueeze(1).to_broadcast([128, 4, N]), op=ALU.mult),
                  reads=[("y", m) for m in range(c0, c0 + 4)] + [("sd", 1)],
                  writes=[("y", m) for m in range(c0, c0 + 4)])
            if ti == NT - 1:
                S.add("pool", lambda e: e.dma_start(out=yT[ti][:, c0 * N:(c0 + 4) * N],
                                                    in_=YB[:, c0:c0 + 4, :].rearrange("p c n -> p (c n)"),
                                                    accum_op=ALU.add, max_dma_last_dim=4096),
                      reads=[("y", m) for m in range(c0, c0 + 4)] + [("yT", ti)], writes=[("yT", ti, c0)], dma=True)
        elif step == 5 and ti == NT - 1:
            S.add("sp", lambda e: e.dma_start(out=o_ffn_tail, in_=TAIL_FLAT),
                  reads=[("tail", j) for j in range(44)], writes=["o8"], dma=True)
        elif step == 5:
            S.add("pool", lambda e: e.dma_start(out=yT[ti], in_=YB.rearrange("p c n -> p (c n)"), accum_op=ALU.add,
                                                max_dma_last_dim=4096),
                  reads=[("y", m) for m in range(8)] + [("yT", ti)], writes=[("yT", ti)], dma=True)
            if ti == NT - 1:
                S.add("sp", lambda e: e.dma_start(out=o_ffn_tail, in_=TAIL_FLAT),
                      reads=[("tail", j) for j in range(44)], writes=["o8"], dma=True)

    set_rr = [0]

    pending = []
    pending_m = []

    def flush_silu():
        while pending:
            pending.pop(0)()

    def flush_mult():
        while pending_m:
            pending_m.pop(0)()

    def flush_pending():
        flush_silu()
        flush_mult()

    def B_pair(ti, j):
        last = (ti == NT - 1)
        hb = H2[ti % 2]
        gb = GB[ti % 2]
        si = set_rr[0]
        set_rr[0] = (si + 1) % NSET
        halves = []
        for half, (blk, fj, T) in enumerate(((2 * j, j, TG[si]), (2 * j + 1, FC + j, TV[si]))):
            bk = next_bank()
            wb = wup_blk(blk)

            def mm(e, bk=bk, wb=wb):
                for k in range(KC):
                    ins = e.matmul(bank(bk, NH), lhsT=wb[:, k, :], rhs=hb[:, k, :], start=(k == 0), stop=(k == KC - 1))
                return ins
            S.add("pe", mm, reads=[("w_up", blk)] + [("h2", ti % 2, k) for k in range(KC)], writes=[("ps", bk)])
            halves.append((bk, fj, T, ("t", si, half), (TG_OFF[si], TV_OFF[si])[half]))
        for bk, fj, T, ttok, toff in halves:
            S.add("act", lambda e, bk=bk, T=T, fj=fj: e.activation(out=T, in_=ps[:, bk * 512:bk * 512 + N], func=AF.Copy,
                                                                    scale=cc(C_WF + fj)),
                  reads=[("ps", bk), "cst"], writes=[ttok], after=alias(toff, N))
        flush_silu()
        for bk, fj, T, ttok, toff in halves:
            S.add("dve", lambda e, bk=bk, T=T, fj=fj: e.scalar_tensor_tensor(
                out=T, in0=ps[:, bk * 512 + 1:bk * 512 + 1 + N], scalar=cc(C_WF + 44 + fj), in1=T,
                op0=ALU.mult, op1=ALU.add),
                reads=[("ps", bk), ttok, "cst"], writes=[ttok])
        if last:
            for bk, fj, T, ttok, toff in halves:
                S.add("pool", lambda e, T=T, fj=fj: e.tensor_copy(out=T[:, SC:N], in_=PRE[:, fj, :]),
                      reads=[ttok, "pre"], writes=[ttok])
        for bk, fj, T, ttok, toff in halves:
            S.add("dve", lambda e, bk=bk, T=T, fj=fj: e.scalar_tensor_tensor(
                out=T, in0=ps[:, bk * 512 + 2:bk * 512 + 2 + N], scalar=cc(C_WF + 88 + fj), in1=T,
                op0=ALU.mult, op1=ALU.add),
                reads=[("ps", bk), ttok, "cst"], writes=[ttok])
        if last:
            for bk, fj, T, ttok, toff in halves:
                pending.append(lambda bk=bk, fj=fj, ttok=ttok: S.add(
                    "act", lambda e: e.activation(out=TAIL[:, fj, :], in_=ps[:, bk * 512 + SC:bk * 512 + NH], func=AF.Copy),
                    reads=[("ps", bk), ttok], writes=[("tail", fj)] + [("x1", k) for k in range(KC)]))
        flush_mult()
        gtok, vtok = halves[0][3], halves[1][3]
        pending.append(lambda si=si: S.add("act", lambda e: e.activation(out=SG[si], in_=TG[si], func=AF.Silu),
                                           reads=[gtok], writes=[gtok]))
        pending_m.append(lambda si=si, j=j: S.add(
            "dve", lambda e: e.tensor_tensor(out=gb[:, j, :], in0=SG[si], in1=TV[si], op=ALU.mult),
            reads=[gtok, vtok], writes=[("g", ti % 2, j)], after=alias(GB_OFF[ti % 2] + j * (N // 2), N // 2)))

    def B_wdown_group(ti, m):
        gb = GB[ti % 2]
        b = next_bank()

        def mm(e):
            for k in range(FC):
                ins = e.matmul(bank(b), lhsT=W_DN[:, m, k, :], rhs=gb[:, k, :], start=(k == 0), stop=(k == FC - 1))
            return ins
        S.add("pe", mm, reads=[("w_dn", m)] + [("g", ti % 2, k) for k in range(FC)], writes=[("ps", b)])
        S.add("act", lambda e: e.activation(out=YB[:, m, :], in_=bank(b), func=AF.Copy, scale=cc(C_G4 + m)),
              reads=[("ps", b), "cst"], writes=[("y", m)], after=alias(YB_OFF + m * N, N))
        S.add("act", lambda e: e.activation(out=YSQ[:, m, :], in_=bank(b), func=AF.Square),
              reads=[("ps", b)], writes=[("ysq", m)], after=alias(YSQ_OFF + m * (N // 2), N // 2))

    def B_epilogue(ti):
        last = (ti == NT - 1)
        norm_stats(YSQ, "ysq", 1)
        S.add("dve", lambda e: e.tensor_tensor(out=YB, in0=YB, in1=bc8(SD[1]), op=ALU.mult),
              reads=[("y", m) for m in range(8)] + [("sd", 1)], writes=[("y", m) for m in range(8)])
        S.add("pool", lambda e: e.dma_start(out=yT[ti], in_=YB.rearrange("p c n -> p (c n)"), accum_op=ALU.add,
                                            max_dma_last_dim=4096),
              reads=[("y", m) for m in range(8)] + [("yT", ti)], writes=[("yT", ti)], dma=True)
        if last:
            S.add("sp", lambda e: e.dma_start(out=o_ffn_tail, in_=TAIL_FLAT),
                  reads=[("tail", j) for j in range(44)], writes=["o8"], dma=True)

    def pre_ffn_state():
        w0b = cc(C_WF, 44).unsqueeze(2).to_broadcast([128, 44, NS])
        w1b = cc(C_WF + 44, 44).unsqueeze(2).to_broadcast([128, 44, NS])
        S.add("dve", lambda e: e.tensor_tensor(out=PRE, in0=ST_FFN[:, :, 0, :], in1=w0b, op=ALU.mult),
              reads=["st_ffn", "cst"], writes=["pre"])
        S.add("dve", lambda e: e.tensor_tensor(out=ST_FFN[:, :, 0, :], in0=ST_FFN[:, :, 1, :], in1=w1b, op=ALU.mult),
              reads=["st_ffn", "cst"], writes=["st_ffn0", "st_ffn"])
        S.add("dve", lambda e: e.tensor_tensor(out=PRE, in0=PRE, in1=ST_FFN[:, :, 0, :], op=ALU.add),
              reads=["pre", "st_ffn0"], writes=["pre"])

    WD_AFTER = {8: 0, 10: 1, 12: 2, 14: 3, 16: 4, 18: 5, 20: 6, 21: 7}
    PRE_AT = {2: [0], 3: [1], 4: [2], 5: [3, 5], 6: [4, 6], 13: [13], 14: [14], 16: [15]}
    for k in range(2, KC):
        PRE_AT[5 + k] = [5 + k]
    EPI_AT = {}
    EPI_NEXT = {0: 0, 1: 1, 2: 2, 3: 3, 4: 4, 5: 5}

    prenorm_A(0)
    for ti in range(NT):
        A_part1(ti)
        if ti > 0:
            A_epilogue(ti - 1)
        A_part2(ti)
        if ti == 1:
            load_states()
    A_epilogue(NT - 1)
    def late_weight_loads():
        b0 = NPF + NWIN
        while b0 < 44:
            nb = min(4, 44 - b0)
            wload(wup_off(b0), w_up, b0 * BL, nb * BL, [("w_up", b0 + i) for i in range(nb)], al=True)
            b0 += nb
        for m in range(8):
            wload(WDN_OFF + m * 1408, w_dn, m * FC * 128, FC * 128, [("w_dn", m)], al=True)
    for ti in range(NT):
        for j in range(FC):
            if j == 0:
                flush_pending()
            B_pair(ti, j)
            if ti > 0 and j in WD_AFTER:
                B_wdown_group(ti - 1, WD_AFTER[j])
            if ti > 0 and j in EPI_AT:
                epilogueB_piece(ti - 1, EPI_AT[j])
            if ti > 1 and j in EPI_NEXT:
                epilogueB_piece(ti - 2, EPI_NEXT[j])
            if ti + 1 < NT and j in PRE_AT:
                for st in PRE_AT[j]:
                    prenormB_piece(ti + 1, st)
            if j == 3:
                y_init(ti)
            if j == 6 and ti == 0:
                late_weight_loads()
            if j == 5 and ti == NT - 2:
                pre_ffn_state()
    flush_pending()
    for st in range(6):
        epilogueB_piece(NT - 2, st)
    for m in range(8):
        B_wdown_group(NT - 1, m)
    for st in range(6):
        epilogueB_piece(NT - 1, st)

    S.add("sp", None, extra=list(S.dma_ops))

    S.finalize(nc, stack)
    with nc.Block() as block:
        @block.tensor
        def _(e):
            S.emit("pe", e)

        @block.scalar
        def _(e):
            S.emit("act", e)

        @block.vector
        def _(e):
            S.emit("dve", e)

        @block.gpsimd
        def _(e):
            S.emit("pool", e)

        @block.sync
        def _(e):
            S.emit("sp", e)
    stack.close()
    return nc


def _blk(w, nk):
    K, C = w.shape
    return np.ascontiguousarray(w.reshape(nk, 128, C // 128, 128).transpose(1, 2, 0, 3))


def _vec(v):
    return v.reshape(-1, 128).T


_NC_CACHE = {}


def kernel(x_prompt, x_sample, state_pool, state_conv, state_ffn, meta_tokens, w_in, w_pool,
           pool_scale, w_conv, w_conv_out, w_o, w_up, w_ffn_conv, w_down, g_pre_mix,
           g_post_mix, g_pre_ffn, g_post_ffn):
    f = np.float32
    x_prompt = np.asarray(x_prompt, f); x_sample = np.asarray(x_sample, f)
    state_pool = np.asarray(state_pool, f); state_conv = np.asarray(state_conv, f)
    state_ffn = np.asarray(state_ffn, f); meta_tokens = np.asarray(meta_tokens, f)
    B = x_prompt.shape[0]

    win = np.asarray(w_in, f)[0]
    win_b = _blk(win, KC)[:, WIN_ORDER]
    wpool = np.asarray(w_pool, f)[0]
    wpool_b = wpool.transpose(1, 0, 2).reshape(128, 4, 2, 128)
    wco_b = _blk(np.asarray(w_conv_out, f)[0], 4)
    wo_b = _blk(np.asarray(w_o, f)[0], KC)
    wup_b = _blk(np.asarray(w_up, f)[0], KC)
    order = []
    for j in range(FC):
        order += [j, FC + j]
    wup_b = wup_b[:, order]
    wdn_b = _blk(np.asarray(w_down, f)[0], FC)
    cst = np.zeros((128, 256), f)
    cst[:, C_G1:C_G1 + 8] = _vec(np.asarray(g_pre_mix, f)[0])
    cst[:, C_G2:C_G2 + 8] = _vec(np.asarray(g_post_mix, f)[0])
    cst[:, C_G3:C_G3 + 8] = _vec(np.asarray(g_pre_ffn, f)[0])
    cst[:, C_G4:C_G4 + 8] = _vec(np.asarray(g_post_ffn, f)[0])
    cst[:, C_PS:C_PS + 8] = _vec(np.asarray(pool_scale, f)[0])
    wc = np.asarray(w_conv, f)[0]
    cst[:, C_WC:C_WC + 12] = wc.reshape(3, 4, 128).transpose(2, 0, 1).reshape(128, 12)
    wf = np.asarray(w_ffn_conv, f)[0]
    cst[:, C_WF:C_WF + 132] = wf.reshape(3, 44, 128).transpose(2, 0, 1).reshape(128, 132)
    for c, w in enumerate(WINS):
        for t in range(16):
            cst[:, C_IC + 16 * c + t] = 1.0 / min(w, t + 1)

    shared = {
        "w_in": np.ascontiguousarray(win_b.reshape(128, -1)),
        "w_pool": np.ascontiguousarray(wpool_b.reshape(128, -1)),
        "w_co": np.ascontiguousarray(wco_b.reshape(128, -1)),
        "w_o": np.ascontiguousarray(wo_b.reshape(128, -1)),
        "w_up": np.ascontiguousarray(wup_b.reshape(128, -1)),
        "w_dn": np.ascontiguousarray(wdn_b.reshape(128, -1)),
        "cst": cst,
    }
    in_maps = []
    for b in range(NCORES):
        sl = slice(NS * b, NS * b + NS)
        X = np.concatenate([meta_tokens, x_prompt[b], x_sample[sl, 0]], axis=0)
        xt = X.reshape(NT, N, KC, 128).transpose(0, 3, 2, 1)
        m = dict(shared)
        m["xT"] = np.ascontiguousarray(xt.reshape(NT, 128, KC * N))
        m["st_pool"] = np.ascontiguousarray(
            state_pool[0, sl].reshape(NS, 15, 4, 128).transpose(3, 2, 1, 0).reshape(128, -1))
        m["st_conv"] = np.ascontiguousarray(
            state_conv[0, sl].reshape(NS, 2, 4, 128).transpose(3, 2, 1, 0).reshape(128, -1))
        m["st_ffn"] = np.ascontiguousarray(
            state_ffn[0, sl].reshape(NS, 2, 44, 128).transpose(3, 2, 1, 0).reshape(128, -1))
        in_maps.append(m)

    if "nc" not in _NC_CACHE:
        _NC_CACHE["nc"] = build_nc()
    nc = _NC_CACHE["nc"]
    res = run_bass_kernel_spmd(nc, in_maps, core_ids=list(range(NCORES)))
    outs = res.results

    NSAMP = NS * NCORES
    y_prompt = np.zeros((B, 2048, D), f)
    y_sample = np.zeros((NSAMP, 1, D), f)
    nsp_p = np.zeros((1, B, 15, 512), f)
    nsc_p = np.zeros((1, B, 2, 512), f)
    nsf_p = np.zeros((1, B, 2, 2 * DFF), f)
    nsp_s = np.zeros((1, NSAMP, 15, 512), f)
    nsc_s = np.zeros((1, NSAMP, 2, 512), f)
    nsf_s = np.zeros((1, NSAMP, 2, 2 * DFF), f)
    for b in range(NCORES):
        r = outs[b]
        sl = slice(NS * b, NS * b + NS)
        Y = r["yT"].reshape(NT, 128, KC, N).transpose(0, 3, 2, 1).reshape(NT * N, D)
        y_prompt[b] = Y[16:16 + 2048]
        y_sample[sl, 0] = Y[16 + 2048:]
        sm = r["o_small"]
        nsp_p[0, b] = sm[:, 0:60].reshape(128, 4, 15).transpose(2, 1, 0).reshape(15, 512)
        nsc_p[0, b] = sm[:, 60:68].reshape(128, 4, 2).transpose(2, 1, 0).reshape(2, 512)
        tail = r["o_ffn_tail"].reshape(128, 44, 18)
        nsf_p[0, b] = tail[:, :, 0:2].transpose(2, 1, 0).reshape(2, 2 * DFF)
        nsf_s[0, sl, 1] = tail[:, :, 2:18].transpose(2, 1, 0).reshape(NS, 2 * DFF)
        nsf_s[0, sl, 0] = r["o_ffn_st"].reshape(128, 44, 2, NS)[:, :, 1, :].transpose(2, 1, 0).reshape(NS, 2 * DFF)
        nsp_s[0, sl, 0:14] = r["o_pool_sa"].reshape(128, 4, 14, NS).transpose(3, 2, 1, 0).reshape(NS, 14, 512)
        nsp_s[0, sl, 14] = sm[:, 68:132].reshape(128, 4, NS).transpose(2, 1, 0).reshape(NS, 512)
        nsc_s[0, sl, 0] = r["o_conv_st"].reshape(128, 4, 2, NS)[:, :, 1, :].transpose(2, 1, 0).reshape(NS, 512)
        nsc_s[0, sl, 1] = sm[:, 132:196].reshape(128, 4, NS).transpose(2, 1, 0).reshape(NS, 512)
    return (y_prompt, y_sample, nsp_p, nsc_p, nsf_p, nsp_s, nsc_s, nsf_s)
```
